# Optimizing a Trainium2 kernel written in Bass

```python
import math
import jax, jax.numpy as jnp
from jax import lax
import numpy as np

D_MODEL = 1024
BATCH = 8
SEQ = 2048
DEPTH = 2
DEC_BATCH = 32
DEC_SEQ = 1
PAST_LEN = 8192
PAGE_SIZE = 128

N_META = 16
N_A_LAYERS = DEPTH // 2
N_B_LAYERS = DEPTH - N_A_LAYERS
SSM_EXPAND = 2
D_INNER = SSM_EXPAND * D_MODEL
SSM_HEAD_DIM = 64
SSM_HEADS = D_INNER // SSM_HEAD_DIM
SSM_GROUPS = 4
SSM_HEADS_PER_GROUP = SSM_HEADS // SSM_GROUPS
SSM_STATE = 128
D_CONV = 4
CONV_DIM = D_INNER + 2 * SSM_GROUPS * SSM_STATE
IN_PROJ_DIM = 2 * D_INNER + 2 * SSM_GROUPS * SSM_STATE + SSM_HEADS
SSD_CHUNK = 128
DIFF_HEADS = 8
DIFF_QK_DIM = 64
DIFF_V_DIM = 2 * DIFF_QK_DIM
ROT_DIM = DIFF_QK_DIM // 4
ROPE_THETA = 500000.0
Q_BLOCK = 128
D_FF = ((8 * D_MODEL + 3 * 256 - 1) // (3 * 256)) * 256
DEEPNORM_ALPHA = (2 * DEPTH) ** 0.25
DEEPNORM_BETA = (8 * DEPTH) ** -0.25
NORM_EPS = 1e-5

kernel_name = "yoco_mamba2_diffattn_decoder_step"


def layer_norm(x, w, b):
    xf = x.astype(jnp.float32)
    mu = jnp.mean(xf, -1, keepdims=True)
    var = jnp.mean(jnp.square(xf - mu), -1, keepdims=True)
    return ((xf - mu) * lax.rsqrt(var + NORM_EPS)).astype(x.dtype) * w + b


def rms_norm(x, w):
    xf = x.astype(jnp.float32)
    return (xf * lax.rsqrt(jnp.mean(jnp.square(xf), -1, keepdims=True) + NORM_EPS)).astype(x.dtype) * w


def post_norm(x, sub, w, b):
    return layer_norm(DEEPNORM_ALPHA * x + sub, w, b)


def swiglu(x, w_gate, w_up, w_down):
    return (jax.nn.silu(x @ w_gate) * (x @ w_up)) @ w_down


def rotary(x, pos):
    half = ROT_DIM // 2
    inv_freq = ROPE_THETA ** (-jnp.arange(half, dtype=jnp.float32) * 2.0 / ROT_DIM)
    ang = pos.astype(jnp.float32)[:, None] * inv_freq[None, :]
    cos = jnp.cos(ang).astype(x.dtype)[None, :, None, None, :]
    sin = jnp.sin(ang).astype(x.dtype)[None, :, None, None, :]
    x1 = x[..., :half]
    x2 = x[..., half:ROT_DIM]
    return jnp.concatenate([x1 * cos - x2 * sin, x2 * cos + x1 * sin, x[..., ROT_DIM:]], -1)


def segsum(x):
    T = x.shape[-1]
    xr = jnp.broadcast_to(x[..., :, None], x.shape + (T,))
    xr = jnp.where(jnp.tril(jnp.ones((T, T), bool), -1), xr, 0.0)
    s = jnp.cumsum(xr, axis=-2)
    return jnp.where(jnp.tril(jnp.ones((T, T), bool)), s, -jnp.inf)


def ssd_chunk_scan(X, dt, A, Bm, Cm, h0, chunk):
    b, L = X.shape[:2]
    c = L // chunk
    G, R, P, N = SSM_GROUPS, SSM_HEADS_PER_GROUP, SSM_HEAD_DIM, SSM_STATE
    dtype = X.dtype
    Xc = (X * dt.astype(dtype)[..., None]).reshape(b, c, chunk, G, R, P)
    ac = (dt * A).reshape(b, c, chunk, G, R).transpose(0, 3, 4, 1, 2)
    Bc = Bm.reshape(b, c, chunk, G, N)
    Cc = Cm.reshape(b, c, chunk, G, N)
    a_cs = jnp.cumsum(ac, axis=-1)
    Lmat = jnp.exp(segsum(ac)).astype(dtype)
    CB = jnp.einsum('bclgn,bcsgn->bgcls', Cc, Bc)
    y_diag = jnp.einsum('bgcls,bgrcls,bcsgrp->bclgrp', CB, Lmat, Xc)
    decay_states = jnp.exp(a_cs[..., -1:] - a_cs).astype(dtype)
    states = jnp.einsum('bcsgn,bgrcs,bcsgrp->bcgrpn', Bc, decay_states, Xc)
    states = jnp.concatenate([h0.astype(dtype).reshape(b, 1, G, R, P, N), states], axis=1)
    chunk_tot = jnp.pad(a_cs[..., -1], ((0, 0), (0, 0), (0, 0), (1, 0)))
    decay_chunk = jnp.exp(segsum(chunk_tot)).astype(dtype)
    new_states = jnp.einsum('bgrzc,bcgrpn->bzgrpn', decay_chunk, states)
    y_off = jnp.einsum('bclgn,bcgrpn,bgrcl->bclgrp', Cc, new_states[:, :-1], jnp.exp(a_cs).astype(dtype))
    y = (y_diag + y_off).reshape(b, L, G * R, P)
    return y, new_states[:, -1].reshape(b, G * R, P, N)


def causal_conv(xBC, prev, conv_w, conv_b):
    full = jnp.concatenate([prev.astype(xBC.dtype), xBC], axis=1)
    out = lax.conv_general_dilated(full, conv_w[:, None, :], window_strides=(1,), padding='VALID',
                                   dimension_numbers=('NWC', 'WIO', 'NWC'),
                                   feature_group_count=CONV_DIM)
    return jax.nn.silu(out + conv_b), full[:, -(D_CONV - 1):]


def mamba2_mixer(h, conv_prev, ssm_prev, segments, w_in, conv_w, conv_b, dt_bias, A_log, D_skip, norm_w, w_out):
    b, L, _ = h.shape
    zxbcdt = h @ w_in
    z = zxbcdt[..., :D_INNER]
    xBC = zxbcdt[..., D_INNER:D_INNER + CONV_DIM]
    dt_raw = zxbcdt[..., D_INNER + CONV_DIM:]
    xBC, new_conv = causal_conv(xBC, conv_prev, conv_w, conv_b)
    GN = SSM_GROUPS * SSM_STATE
    xs = xBC[..., :D_INNER].reshape(b, L, SSM_HEADS, SSM_HEAD_DIM)
    Bm = xBC[..., D_INNER:D_INNER + GN].reshape(b, L, SSM_GROUPS, SSM_STATE)
    Cm = xBC[..., D_INNER + GN:].reshape(b, L, SSM_GROUPS, SSM_STATE)
    dt = jax.nn.softplus((dt_raw + dt_bias).astype(jnp.float32))
    A = -jnp.exp(A_log.astype(jnp.float32))
    state = ssm_prev
    ys = []
    start = 0
    for length, chunk in segments:
        y_seg, state = ssd_chunk_scan(xs[:, start:start + length], dt[:, start:start + length], A,
                                      Bm[:, start:start + length], Cm[:, start:start + length], state, chunk)
        ys.append(y_seg)
        start += length
    y = jnp.concatenate(ys, axis=1) + xs * D_skip[:, None]
    y = rms_norm(y.reshape(b, L, D_INNER) * jax.nn.silu(z), norm_w)
    return y @ w_out, new_conv, state


def shared_kv(h, pos, w_k, w_v):
    b, L, _ = h.shape
    k = rotary((h @ w_k).reshape(b, L, DIFF_HEADS, 2, DIFF_QK_DIM), pos)
    v = (h @ w_v).reshape(b, L, DIFF_HEADS, DIFF_V_DIM)
    return k, v


def diff_lambda(lam_vecs, lambda_init):
    lv = lam_vecs.astype(jnp.float32)
    return jnp.exp(jnp.sum(lv[0] * lv[1])) - jnp.exp(jnp.sum(lv[2] * lv[3])) + lambda_init


def diff_weights(s, lam):
    p = jax.nn.softmax(s, axis=-1)
    return p[:, :, 0] - lam * p[:, :, 1]


def diff_output(o, subln_w, w_o, lambda_init):
    b, L = o.shape[:2]
    o = rms_norm(o, subln_w) * (1.0 - lambda_init)
    return o.reshape(b, L, DIFF_HEADS * DIFF_V_DIM) @ w_o


def diff_attn_prompt(h, k, v, pos, w_q, lam_vecs, subln_w, w_o, lambda_init):
    b, L, _ = h.shape
    q = rotary((h @ w_q).reshape(b, L, DIFF_HEADS, 2, DIFF_QK_DIM), pos)
    lam = diff_lambda(lam_vecs, lambda_init)
    nb = -(-L // Q_BLOCK)
    Lp = nb * Q_BLOCK
    pad = Lp - L
    qp = jnp.pad(q, ((0, 0), (0, pad), (0, 0), (0, 0), (0, 0)))
    kp = jnp.pad(k, ((0, 0), (0, pad), (0, 0), (0, 0), (0, 0)))
    vp = jnp.pad(v, ((0, 0), (0, pad), (0, 0), (0, 0)))
    q_blocks = qp.reshape(b, nb, Q_BLOCK, DIFF_HEADS, 2, DIFF_QK_DIM).swapaxes(0, 1)
    key_idx = jnp.arange(Lp)
    scale = DIFF_QK_DIM ** -0.5

    def attend_block(args):
        qb, blk = args
        s = jnp.einsum('bqhcd,bkhcd->bhcqk', qb, kp).astype(jnp.float32) * scale
        q_idx = blk * Q_BLOCK + jnp.arange(Q_BLOCK)
        s = jnp.where(key_idx[None, :] <= q_idx[:, None], s, -jnp.inf)
        w = diff_weights(s, lam).astype(vp.dtype)
        return jnp.einsum('bhqk,bkhe->bqhe', w, vp)

    o = lax.map(attend_block, (q_blocks, jnp.arange(nb)))
    o = o.swapaxes(0, 1).reshape(b, Lp, DIFF_HEADS, DIFF_V_DIM)[:, :L]
    return diff_output(o, subln_w, w_o, lambda_init)


def diff_attn_sample(h, k_past, v_past, k_new, v_new, pos, w_q, lam_vecs, subln_w, w_o, lambda_init):
    b, T, _ = h.shape
    q = rotary((h @ w_q).reshape(b, T, DIFF_HEADS, 2, DIFF_QK_DIM), pos)
    lam = diff_lambda(lam_vecs, lambda_init)
    scale = DIFF_QK_DIM ** -0.5
    s_past = jnp.einsum('bqhcd,bkhcd->bhcqk', q, k_past).astype(jnp.float32) * scale
    s_new = jnp.einsum('bqhcd,bkhcd->bhcqk', q, k_new).astype(jnp.float32) * scale
    s_new = jnp.where(jnp.tril(jnp.ones((T, T), bool)), s_new, -jnp.inf)
    w = diff_weights(jnp.concatenate([s_past, s_new], axis=-1), lam).astype(v_new.dtype)
    P = k_past.shape[1]
    o = (jnp.einsum('bhqk,bkhe->bqhe', w[..., :P], v_past)
         + jnp.einsum('bhqk,bkhe->bqhe', w[..., P:], v_new))
    return diff_output(o, subln_w, w_o, lambda_init)


def setup_inputs(seed: int = 0) -> dict:
    key = jax.random.key(seed)
    ks = jax.random.split(key, 32)
    n_pages = PAST_LEN // PAGE_SIZE
    n_pool = (5 * DEC_BATCH * n_pages) // 4
    HQK = DIFF_HEADS * 2 * DIFF_QK_DIM
    HV = DIFF_HEADS * DIFF_V_DIM

    def nrm(k, shape, scale):
        return jax.random.normal(k, shape, jnp.float32) * scale

    dt0 = jnp.exp(jax.random.uniform(ks[11], (N_A_LAYERS, SSM_HEADS), jnp.float32,
                                     minval=math.log(1e-3), maxval=math.log(1e-1)))
    return {
        "x_prompt": nrm(ks[0], (BATCH, SEQ, D_MODEL), 1.0),
        "x_sample": nrm(ks[1], (DEC_BATCH, DEC_SEQ, D_MODEL), 1.0),
        "cache_k": nrm(ks[2], (n_pool, PAGE_SIZE, DIFF_HEADS, 2, DIFF_QK_DIM), 1.0),
        "cache_v": nrm(ks[3], (n_pool, PAGE_SIZE, DIFF_HEADS, DIFF_V_DIM), 1.0),
        "state_ssm": nrm(ks[4], (N_A_LAYERS, DEC_BATCH, SSM_HEADS, SSM_HEAD_DIM, SSM_STATE), 0.1),
        "state_conv": nrm(ks[5], (N_A_LAYERS, DEC_BATCH, D_CONV - 1, CONV_DIM), 1.0),
        "page_table": jax.random.permutation(ks[6], n_pool)[:DEC_BATCH * n_pages]
                      .reshape(DEC_BATCH, n_pages).astype(jnp.int32),
        "meta_tokens": nrm(ks[7], (N_META, D_MODEL), 1.0),
        "a_w_in": nrm(ks[8], (N_A_LAYERS, D_MODEL, IN_PROJ_DIM), D_MODEL ** -0.5),
        "a_conv_w": nrm(ks[9], (N_A_LAYERS, D_CONV, CONV_DIM), D_CONV ** -0.5),
        "a_conv_b": nrm(ks[10], (N_A_LAYERS, CONV_DIM), 0.02),
        "a_dt_bias": dt0 + jnp.log(-jnp.expm1(-dt0)),
        "a_A_log": jnp.log(jax.random.uniform(ks[12], (N_A_LAYERS, SSM_HEADS), jnp.float32, 1.0, 16.0)),
        "a_D": 1.0 + nrm(ks[13], (N_A_LAYERS, SSM_HEADS), 0.1),
        "a_norm_w": 1.0 + nrm(ks[14], (N_A_LAYERS, D_INNER), 0.02),
        "a_w_out": nrm(ks[15], (N_A_LAYERS, D_INNER, D_MODEL), D_INNER ** -0.5 * DEEPNORM_BETA),
        "kv_w_k": nrm(ks[16], (D_MODEL, HQK), D_MODEL ** -0.5),
        "kv_w_v": nrm(ks[17], (D_MODEL, HV), D_MODEL ** -0.5),
        "b_w_q": nrm(ks[18], (N_B_LAYERS, D_MODEL, HQK), D_MODEL ** -0.5),
        "b_lambda": nrm(ks[19], (N_B_LAYERS, 4, DIFF_QK_DIM), 0.1),
        "b_subln_w": 1.0 + nrm(ks[20], (N_B_LAYERS, DIFF_V_DIM), 0.02),
        "b_w_o": nrm(ks[21], (N_B_LAYERS, HV, D_MODEL), HV ** -0.5 * DEEPNORM_BETA),
        "ffn_w_gate": nrm(ks[22], (DEPTH, D_MODEL, D_FF), D_MODEL ** -0.5),
        "ffn_w_up": nrm(ks[23], (DEPTH, D_MODEL, D_FF), D_MODEL ** -0.5),
        "ffn_w_down": nrm(ks[24], (DEPTH, D_FF, D_MODEL), D_FF ** -0.5 * DEEPNORM_BETA),
        "ln_mix_w": 1.0 + nrm(ks[25], (DEPTH, D_MODEL), 0.02),
        "ln_mix_b": nrm(ks[26], (DEPTH, D_MODEL), 0.01),
        "ln_ffn_w": 1.0 + nrm(ks[27], (DEPTH, D_MODEL), 0.02),
        "ln_ffn_b": nrm(ks[28], (DEPTH, D_MODEL), 0.01),
    }


def reference(x_prompt, x_sample, cache_k, cache_v, state_ssm, state_conv, page_table, meta_tokens,
              a_w_in, a_conv_w, a_conv_b, a_dt_bias, a_A_log, a_D, a_norm_w, a_w_out,
              kv_w_k, kv_w_v, b_w_q, b_lambda, b_subln_w, b_w_o,
              ffn_w_gate, ffn_w_up, ffn_w_down, ln_mix_w, ln_mix_b, ln_ffn_w, ln_ffn_b):
    bp, seq, _ = x_prompt.shape
    bs, T, _ = x_sample.shape
    n_pages = page_table.shape[1]
    past_len = n_pages * PAGE_SIZE

    hp = jnp.concatenate([jnp.broadcast_to(meta_tokens.astype(x_prompt.dtype)[None], (bp, N_META, D_MODEL)),
                          x_prompt], axis=1)
    hs = x_sample
    L_tot = hp.shape[1]
    pos_p = jnp.arange(L_tot)
    pos_s = past_len + jnp.arange(T)
    segs_p = ((N_META, N_META), (seq, min(SSD_CHUNK, seq)))
    segs_s = ((T, T),)

    k_past = cache_k[page_table].reshape(bs, past_len, DIFF_HEADS, 2, DIFF_QK_DIM)
    v_past = cache_v[page_table].reshape(bs, past_len, DIFF_HEADS, DIFF_V_DIM)

    ssm_p_list, conv_p_list, ssm_s_list, conv_s_list = [], [], [], []
    for l in range(DEPTH):
        if l < N_A_LAYERS:
            params = (a_w_in[l], a_conv_w[l], a_conv_b[l], a_dt_bias[l], a_A_log[l], a_D[l], a_norm_w[l], a_w_out[l])
            conv0 = jnp.zeros((bp, D_CONV - 1, CONV_DIM), hp.dtype)
            ssm0 = jnp.zeros((bp, SSM_HEADS, SSM_HEAD_DIM, SSM_STATE), hp.dtype)
            mix_p, conv_p, ssm_p = mamba2_mixer(hp, conv0, ssm0, segs_p, *params)
            mix_s, conv_s, ssm_s = mamba2_mixer(hs, state_conv[l], state_ssm[l], segs_s, *params)
            ssm_p_list.append(ssm_p)
            conv_p_list.append(conv_p)
            ssm_s_list.append(ssm_s)
            conv_s_list.append(conv_s)
        else:
            j = l - N_A_LAYERS
            lambda_init = 0.8 - 0.6 * math.exp(-0.3 * l)
            mix_p = diff_attn_prompt(hp, k_prompt, v_prompt, pos_p, b_w_q[j], b_lambda[j], b_subln_w[j],
                                     b_w_o[j], lambda_init)
            mix_s = diff_attn_sample(hs, k_past, v_past, k_sample, v_sample, pos_s, b_w_q[j], b_lambda[j],
                                     b_subln_w[j], b_w_o[j], lambda_init)
        hp = post_norm(hp, mix_p, ln_mix_w[l], ln_mix_b[l])
        hs = post_norm(hs, mix_s, ln_mix_w[l], ln_mix_b[l])
        hp = post_norm(hp, swiglu(hp, ffn_w_gate[l], ffn_w_up[l], ffn_w_down[l]), ln_ffn_w[l], ln_ffn_b[l])
        hs = post_norm(hs, swiglu(hs, ffn_w_gate[l], ffn_w_up[l], ffn_w_down[l]), ln_ffn_w[l], ln_ffn_b[l])
        if l == N_A_LAYERS - 1:
            k_prompt, v_prompt = shared_kv(hp, pos_p, kv_w_k, kv_w_v)
            k_sample, v_sample = shared_kv(hs, pos_s, kv_w_k, kv_w_v)

    y_prompt = hp[:, N_META:]
    ssm_prompt = jnp.stack(ssm_p_list, 0)
    conv_prompt = jnp.stack(conv_p_list, 0)
    ssm_sample = jnp.stack(ssm_s_list, 0)
    conv_sample = jnp.stack(conv_s_list, 0)
    return (y_prompt, hs, k_prompt, v_prompt, ssm_prompt, conv_prompt, k_sample, v_sample, ssm_sample, conv_sample)
```

```python
import contextlib
import math
import numpy as np
import concourse.bass as bass
import concourse.mybir as mybir
from concourse.bass_utils import run_bass_kernel_spmd

F32 = mybir.dt.float32
BF16 = mybir.dt.bfloat16
I32 = mybir.dt.int32
AF = mybir.ActivationFunctionType
ALU = mybir.AluOpType
AX = mybir.AxisListType

PE, ACT, DVE, POOL, SP = "pe", "act", "dve", "pool", "sp"
ENGS = [PE, ACT, DVE, POOL, SP]
N_DMA_SLOTS = 12

NCORES = 8
D = 1024
SEQ = 2048
NMETA = 16
LTOT = SEQ + NMETA
DI = 2048
NH = 32
HP = 64
NG = 4
NS = 128
CONVD = 3072
INP = 5152
DFF = 2816
NSAMP = 4
PAST = 8192
NPAGES = 64
NPOOL = 2560
ALPHA = 4 ** 0.25
EPS = 1e-5
LAMBDA_INIT = 0.8 - 0.6 * math.exp(-0.3 * 1)
CHS = [384, 384, 384, 384, 384, 128]
NCHUNK = len(CHS)
XS0 = [sum(CHS[:i]) for i in range(NCHUNK)]
TC = NMETA + max(CHS) + NSAMP
SCALE = 64 ** -0.5


class Buf:
    __slots__ = ("name", "w", "rs")

    def __init__(self, name):
        self.name = name
        self.w = None
        self.rs = []


class Sched:
    def __init__(self, nc):
        self.nc = nc
        self.recs = {e: [] for e in ENGS}
        self.waited = {e: {} for e in ENGS}
        self.dma_slot_next = {e: 0 for e in ENGS}
        self.dma_slot_uses = {}
        self.nbuf = 0

    def buf(self, name=None):
        self.nbuf += 1
        return Buf(name or f"b{self.nbuf}")

    def _need(self, eng, tok, waits):
        if tok is None:
            return
        if tok[0] == "c":
            _, e2, i2 = tok
            if e2 == PE and eng == PE:
                return
            key = ("c", e2)
            val = i2
        else:
            _, e2, slot, use = tok
            key = ("d", e2, slot)
            val = use
        if self.waited[eng].get(key, -1) >= val:
            return
        for j, (k, v, t) in enumerate(waits):
            if k == key:
                if v < val:
                    waits[j] = (key, val, tok)
                return
        waits.append((key, val, tok))

    def add(self, eng, fn, reads=(), writes=(), dma=False):
        waits = []
        for b in reads:
            self._need(eng, b.w, waits)
        for b in writes:
            self._need(eng, b.w, waits)
            for r in b.rs:
                self._need(eng, r, waits)
        idx = len(self.recs[eng])
        if dma:
            slot = self.dma_slot_next[eng]
            self.dma_slot_next[eng] = (slot + 1) % N_DMA_SLOTS
            use = self.dma_slot_uses.get((eng, slot), 0) + 1
            self.dma_slot_uses[(eng, slot)] = use
            if use > 1:
                self._need(eng, ("d", eng, slot, use - 1), waits)
            tok = ("d", eng, slot, use)
        else:
            tok = ("c", eng, idx)
        for (key, val, t) in waits:
            self.waited[eng][key] = val
        self.recs[eng].append(dict(fn=fn, waits=[t for (_, _, t) in waits], tok=tok, sig=False))
        for b in reads:
            b.rs.append(tok)
        for b in writes:
            b.w = tok
            b.rs = []
        return tok

    def emit(self, final_bufs=()):
        nc = self.nc
        fin_waits = []
        for b in final_bufs:
            self._need(SP, b.w, fin_waits)
        fin = [t for (_, _, t) in fin_waits]
        for e in ENGS:
            for rec in self.recs[e]:
                for t in rec["waits"]:
                    if t[0] == "c":
                        self.recs[t[1]][t[2]]["sig"] = True
        for t in fin:
            if t[0] == "c":
                self.recs[t[1]][t[2]]["sig"] = True
        cnt = {}
        for e in ENGS:
            c = 0
            for i, rec in enumerate(self.recs[e]):
                if rec["tok"][0] == "c" and rec["sig"]:
                    c += 1
                cnt[(e, i)] = c
        with contextlib.ExitStack() as st:
            csem = {e: st.enter_context(nc.semaphore(f"cs_{e}")) for e in ENGS}
            dsem = {}
            for e in ENGS:
                for s in range(N_DMA_SLOTS):
                    if (e, s) in self.dma_slot_uses:
                        dsem[(e, s)] = st.enter_context(nc.semaphore(f"ds_{e}{s}"))
            block = st.enter_context(nc.Block())

            def wait_tok(engine, t):
                if t[0] == "c":
                    engine.wait_ge(csem[t[1]], cnt[(t[1], t[2])])
                else:
                    engine.wait_ge(dsem[(t[1], t[2])], 16 * t[3])

            def run(ename, engine, extra_final=None):
                for rec in self.recs[ename]:
                    for t in rec["waits"]:
                        wait_tok(engine, t)
                    ins = rec["fn"](engine)
                    tok = rec["tok"]
                    if tok[0] == "d":
                        ins.then_inc(dsem[(tok[1], tok[2])], 16)
                    elif rec["sig"]:
                        ins.then_inc(csem[ename], 1)
                if extra_final:
                    for t in extra_final:
                        wait_tok(engine, t)

            @block.tensor
            def _(eng):
                run(PE, eng)

            @block.scalar
            def _(eng):
                run(ACT, eng)

            @block.vector
            def _(eng):
                run(DVE, eng)

            @block.gpsimd
            def _(eng):
                run(POOL, eng)

            @block.sync
            def _(eng):
                run(SP, eng, fin)


class T:
    def __init__(self, t, b):
        self.t = t
        self.b = b

    def __getitem__(self, k):
        return self.t[k]


def build_program(stage=99, npool=NPOOL):
    nc = bass.Bass("TRN2", target_bir_lowering=False)
    S = Sched(nc)
    st = contextlib.ExitStack()

    def din(name, shape, dt=F32):
        return nc.dram_tensor(name, list(shape), dt, kind="ExternalInput").ap()

    def dout(name, shape, dt=F32):
        return T(nc.dram_tensor(name, list(shape), dt, kind="ExternalOutput").ap(), S.buf(name))

    def dscr(name, shape, dt):
        return T(nc.dram_tensor(name, list(shape), dt, kind="Internal").ap(), S.buf(name))

    def sb(name, shape, dt=F32):
        return T(st.enter_context(nc.sbuf_tensor(name, list(shape), dt)), S.buf(name))

    x_p = din("x_p", [SEQ, D])
    x_s = din("x_s", [NSAMP, D])
    meta = din("meta", [NMETA, D])
    if stage >= 3:
        TT_ = 4
        NTT = 128 // TT_
        cache_k = din("cache_k", [npool * NTT, TT_ * D])
        cache_v = din("cache_v", [npool * NTT, TT_ * D])
        ptab = din("ptab", [NSAMP * NPAGES, 1], I32)
        selp_in = din("selp", [128, 2])
        cmask_in = din("cmask", [4, 2])
        comb_in = din("comb", [2, 4, 2])
    st_ssm = din("st_ssm", [NSAMP, NH * HP, NS])
    st_conv = din("st_conv", [NSAMP * 3, CONVD])
    w_in = din("w_in", [D, INP])
    conv_w = din("conv_w", [4, CONVD])
    conv_b = din("conv_b", [CONVD, 1])
    dt_bias = din("dt_bias", [NH, 1])
    a_log = din("a_log", [NH, 1])
    d_skip = din("d_skip", [NH, 1])
    norm_w = din("norm_w", [DI, 1])
    w_out = din("w_out", [DI, D])
    w_k = din("w_k", [D, D])
    w_v = din("w_v", [D, D])
    w_q = din("w_q", [D, D])
    lam = din("lam", [1, 256])
    subw = din("subw", [1, 128])
    w_o = din("w_o", [D, D])
    w_g = din("w_g", [2, D, DFF])
    w_u = din("w_u", [2, D, DFF])
    w_d = din("w_d", [2, DFF, D])
    ln_w = din("ln_w", [4, D, 1])
    ln_b = din("ln_b", [4, D, 1])
    rope_p = din("rope_p", [LTOT, 16])
    rope_s = din("rope_s", [NSAMP, 16])
    masks_in = din("masks", [3, 128, 128])

    y_p = dout("y_p", [SEQ, D])
    y_s = dout("y_s", [NSAMP, D])
    k_p = dout("k_p", [LTOT, D])
    v_p = dout("v_p", [LTOT, D])
    ssm_p = dout("ssm_p", [NH * HP, NS])
    conv_p = dout("conv_p", [3, CONVD])
    k_s = dout("k_s", [NSAMP, D])
    v_s = dout("v_s", [NSAMP, D])
    ssm_s = dout("ssm_s", [NSAMP, NH * HP, NS])
    conv_s = dout("conv_s", [NSAMP * 3, CONVD])

    kT_scr = dscr("kT_scr", [8, 128, LTOT + 112], BF16)
    va_scr = dscr("va_scr", [LTOT + 112, 8 * 129], BF16)
    q_scr = dscr("q_scr", [NSAMP, D], F32)
    acs_scr = [dscr(f"acs_scr{i}", [NH, TC], F32) for i in range(2)]

    banks = []
    for i in range(8):
        t = st.enter_context(nc.psum_tensor(f"bank{i}", [128, 512], F32))
        banks.append(T(t, S.buf(f"bank{i}")))
    bank_i = [0]
    nb_pool = [8]

    def nb():
        b = banks[bank_i[0] % nb_pool[0]]
        bank_i[0] += 1
        return b

    ident = sb("ident", [128, 128], F32)
    identb = sb("identb", [128, 128], BF16)
    onesb = sb("onesb", [128, 128], BF16)
    masks = sb("masks_sb", [128, 3, 128], BF16)
    hT = sb("hT", [128, 8, TC], BF16)
    u = sb("u", [128, 8, TC], F32)
    bigA = sb("bigA", [128, 24, TC + 3], BF16)
    bigB = sb("bigB", [128, 16, TC], BF16)
    bigC = sb("bigC", [128, 16, TC], BF16)
    ysq = T(bigA.t[:, 0:16, 0:TC], bigA.b)
    ubf = T(bigA.t[:, 0:8, 0:TC], bigA.b)
    usq = T(bigA.t[:, 8:16, 0:TC], bigA.b)
    BT = sb("BT", [128, 4, TC], BF16)
    CT = sb("CT", [128, 4, TC], BF16)
    NWB = 3
    WCOLS = 5632
    wbufs = [sb(f"wbuf{i}", [128, WCOLS], BF16) for i in range(NWB)]
    wb_i = [0]
    st1 = sb("st1", [128, TC], F32)
    st3 = sb("st3", [128, TC], F32)
    tmpA = sb("tmpA", [128, 512], F32)
    st2 = T(tmpA.t[:, 0:TC], tmpA.b)
    tmpBs = [sb(f"tmpB{i}", [128, 512], F32) for i in range(2)]
    ktm = sb("ktm", [128, D], F32)
    xtm = ktm
    ktmb = sb("ktmb", [128, D], BF16)
    vtm = sb("vtm", [128, D], F32)
    vaug = sb("vaug", [128, 8, 129], BF16)
    kTt = sb("kTt", [128, 8, 128], BF16)
    rope_t = sb("rope_t", [128, 16], F32)
    rtmp = sb("rtmp", [128, 16, 8, 4], F32)
    lnw = sb("lnw", [128, 4, 8], F32)
    lnb = sb("lnb", [128, 4, 8], F32)
    convw = sb("convw", [128, 24, 4], F32)
    convb = sb("convb", [128, 24], F32)
    normw = sb("normw", [128, 16], F32)
    dfm = sb("dfm", [128, 16], F32)
    dtb = sb("dtb", [32, 1], F32)
    aneg = sb("aneg", [32, 1], F32)
    lam_t = T(tmpA.t[:, 0:256], tmpA.b)
    lam_v = sb("lam_v", [128, 4], F32)
    subw_t = sb("subw_t", [128, 128], F32)
    diag = [sb(f"diag{i}", [128, 4, 128], BF16) for i in range(2)]
    dt_t = sb("dt_t", [32, TC], F32)
    a_t = sb("a_t", [32, TC], F32)
    acs_t = sb("acs_t", [32, TC], F32)
    ones32 = sb("ones32", [32, TC], F32)
    dtk = sb("dtk", [128, 32], F32)
    acsk = sb("acsk", [128, 32], F32)
    nacsk = sb("nacsk", [128, 32], F32)
    acsl = sb("acsl", [128, 32], F32)
    etot = sb("etot", [128, 32], F32)
    w2 = sb("w2", [128, 32], F32)
    xcs = sb("xcs", [128, DI], BF16)
    xds = sb("xds", [128, DI], BF16)
    btok = sb("btok", [128, 512], BF16)
    cbm = sb("cbm", [128, 4, 128], BF16)
    rbc = [sb(f"rbc{i}", [128, 8, 128], F32) for i in range(2)]
    rm = sb("rm", [128, 8, 128], F32)
    mts = [sb(f"mt{i}", [128, 8, 128], BF16) for i in range(2)]
    ctss = [sb(f"cts{i}", [128, 8, 128], BF16) for i in range(2)]
    s32 = sb("s32", [128, NH, HP], F32)
    sbf = sb("sbf", [128, NH, HP], BF16)
    s32g = [S.buf(f"s32g{i}") for i in range(4)]
    sbfg = [S.buf(f"sbfg{i}") for i in range(4)]
    stl = T(u.t[:, :, :].rearrange("p a b -> p (a b)")[:, 0:2048].rearrange("p (j n) -> p j n", j=16), u.b)
    convst = sb("convst", [128, 24, 4], F32)
    ctail = sb("ctail", [128, 24, 3], BF16)
    xpre_s = sb("xpre_s", [128, 24, NSAMP, 4], BF16)
    kTh = [sb(f"kTh{i}", [128, LTOT + 112], BF16) for i in range(2)]
    vah = [sb(f"vah{i}", [128, 17, 129], BF16) for i in range(2)]
    et = [sb(f"et{i}", [128, 4, 128], BF16) for i in range(3)]
    et_i = [0]
    ob_i = [0]
    ofin = sb("ofin", [128, 128], F32)
    ofbs = [sb(f"ofb{i}", [128, 128], BF16) for i in range(2)]
    ofb_i = [0]
    epsc = sb("epsc", [128, 1], F32)
    pend_tr = [None]
    sm = sb("sm", [128, 8], F32)
    junk = sb("junk", [128, 128], F32)
    hs1 = sb("hs1", [128, 8, NSAMP], BF16)
    yfm = u
    ytm = ktm

    def dma(q, out, in_, reads, writes, **kw):
        S.add(q, lambda e: e.dma_start(out=out, in_=in_, allow_slow_non_contiguous=True, **kw), reads=reads, writes=writes, dma=True)

    def slow(q, out, in_, reads, writes):
        S.add(q, lambda e: e.dma_start(out=out, in_=in_, allow_slow_non_contiguous=True), reads=reads, writes=writes, dma=True)

    def act(out, in_, func, r, w, bias=None, scale=None, accum=None):
        kw = {}
        if bias is not None:
            kw["bias"] = bias
        if scale is not None:
            kw["scale"] = scale
        if accum is not None:
            kw["accum_out"] = accum
        S.add(ACT, lambda e: e.activation(out=out, in_=in_, func=func, **kw), reads=r, writes=w)

    def acopy(out, in_, r, w):
        S.add(ACT, lambda e: e.copy(out=out, in_=in_), reads=r, writes=w)

    def vcopy(out, in_, r, w):
        S.add(DVE, lambda e: e.tensor_copy(out=out, in_=in_), reads=r, writes=w)

    def tt(out, in0, in1, op, r, w):
        S.add(DVE, lambda e: e.tensor_tensor(out=out, in0=in0, in1=in1, op=op), reads=r, writes=w)

    def ts(out, in0, s1, op0, r, w, s2=None, op1=None):
        if op1 is None:
            S.add(DVE, lambda e: e.tensor_scalar(out=out, in0=in0, scalar1=s1, scalar2=None, op0=op0), reads=r, writes=w)
        else:
            S.add(DVE, lambda e: e.tensor_scalar(out=out, in0=in0, scalar1=s1, scalar2=s2, op0=op0, op1=op1), reads=r, writes=w)

    def stt(out, in0, scalar, in1, op0, op1, r, w):
        S.add(DVE, lambda e: e.scalar_tensor_tensor(out=out, in0=in0, scalar=scalar, in1=in1, op0=op0, op1=op1), reads=r, writes=w)

    def mm(out, lhsT, rhs, start, stop, r, w):
        S.add(PE, lambda e: e.matmul(out, lhsT=lhsT, rhs=rhs, start=start, stop=stop), reads=r, writes=w)

    def tr(out, in_, idn, r, w):
        S.add(PE, lambda e: e.transpose(out, in_, idn), reads=r, writes=w)

    def memset(eng, ap, val, w):
        S.add(eng, lambda e: e.memset(ap, val), writes=w)

    def bcast(t, off, dims):
        return bass.AP(t, off, dims)

    def psb(bank):
        return bank.t[:].bitcast(BF16)

    memset(DVE, ident[:], 0.0, [ident.b])
    S.add(POOL, lambda e: e.affine_select(out=ident[:], in_=ident[:], pattern=[[-1, 128]], compare_op=ALU.not_equal,
                                          fill=1.0, base=0, channel_multiplier=1), reads=[ident.b], writes=[ident.b])
    vcopy(identb[:], ident[:], [ident.b], [identb.b])
    memset(DVE, onesb[:], 1.0, [onesb.b])
    memset(DVE, epsc[:], EPS, [epsc.b])
    memset(DVE, ones32[:], 1.0, [ones32.b])
    dma(POOL, masks[:], masks_in.rearrange("m r j -> r m j"), [], [masks.b])
    for l_ in range(4):
        slow(SP, lnw[:, l_, :], ln_w[l_].rearrange("(k p) o -> p (k o)", p=128), [], [lnw.b])
        slow(SP, lnb[:, l_, :], ln_b[l_].rearrange("(k p) o -> p (k o)", p=128), [], [lnb.b])
    for k_ in range(4):
        slow(SP, convw[:, :, k_], conv_w[k_, :].rearrange("(j p) -> p j", p=128), [], [convw.b])
    slow(SP, convb[:], conv_b.rearrange("(j p) o -> p (j o)", p=128), [], [convb.b])
    slow(SP, normw[:], norm_w.rearrange("(j p) o -> p (j o)", p=128), [], [normw.b])
    dma(SP, dtb[:], dt_bias, [], [dtb.b])
    dma(SP, aneg[:], a_log, [], [aneg.b])
    act(aneg[:], aneg[:], AF.Exp, [aneg.b], [aneg.b])
    ts(aneg[:], aneg[:], -1.0, ALU.mult, [aneg.b], [aneg.b])
    for half in range(2):
        src = bass.AP(d_skip.tensor, half, [[0, 64], [2, 16]])
        slow(SP, dfm[64 * half:64 * half + 64, :], src, [], [dfm.b])
    dma(SP, lam_t[:], bass.AP(lam.tensor, 0, [[0, 128], [1, 256]]), [], [lam_t.b])
    dma(SP, subw_t[:], bass.AP(subw.tensor, 0, [[0, 128], [1, 128]]), [], [subw_t.b])
    ts(subw_t[:], subw_t[:], 1.0 - LAMBDA_INIT, ALU.mult, [subw_t.b], [subw_t.b])
    tt(junk[:, 0:64], lam_t[:, 0:64], lam_t[:, 64:128], ALU.mult, [lam_t.b], [junk.b])
    S.add(DVE, lambda e: e.tensor_reduce(out=lam_v[:, 0:1], in_=junk[:, 0:64], axis=AX.X, op=ALU.add), reads=[junk.b], writes=[lam_v.b])
    tt(junk[:, 64:128], lam_t[:, 128:192], lam_t[:, 192:256], ALU.mult, [lam_t.b], [junk.b])
    S.add(DVE, lambda e: e.tensor_reduce(out=lam_v[:, 1:2], in_=junk[:, 64:128], axis=AX.X, op=ALU.add), reads=[junk.b], writes=[lam_v.b])
    act(lam_v[:, 0:2], lam_v[:, 0:2], AF.Exp, [lam_v.b], [lam_v.b])
    tt(lam_v[:, 2:3], lam_v[:, 0:1], lam_v[:, 1:2], ALU.subtract, [lam_v.b], [lam_v.b])
    ts(lam_v[:, 3:4], lam_v[:, 2:3], LAMBDA_INIT, ALU.add, [lam_v.b], [lam_v.b], s2=-1.0, op1=ALU.mult)

    wscr = {}
    pend_wr = [None]

    def load_w(wdram, r0, nrows, c0, ncols):
        kt = nrows // 128
        wb = wbufs[wb_i[0] % NWB]
        wb_i[0] += 1
        flat = wb.t[:, 0:kt * ncols]
        view = flat.rearrange("p (k c) -> p k c", k=kt)
        key = (wdram.tensor.name, int(wdram.offset), r0, nrows, c0, ncols)
        if key in wscr:
            if pend_wr[0] is not None:
                pend_wr[0]()
                pend_wr[0] = None
            scr = wscr[key]
            dma(POOL, flat, scr.t[:, :], [scr.b], [wb.b])
        else:
            src = wdram[r0:r0 + nrows, c0:c0 + ncols].rearrange("(k p) c -> p k c", p=128)
            dma(POOL, view, src, [], [wb.b])
            scr = dscr(f"wscr{len(wscr)}", [128, kt * ncols], BF16)
            wscr[key] = scr
            prev = pend_wr[0]
            pend_wr[0] = lambda flat=flat, scr=scr, wb=wb: dma(POOL, scr.t[:, :], flat, [wb.b], [scr.b])
            if prev is not None:
                prev()
        return view, wb.b

    def flush_wr():
        if pend_wr[0] is not None:
            pend_wr[0]()
            pend_wr[0] = None

    def rows_to_fm(src_rows, nrows, col0, dst, dstb, extra_reads=()):
        dma(SP, xtm[0:nrows, :], src_rows, list(extra_reads), [xtm.b])
        for half in range(2):
            bk = nb()
            for kk in range(4):
                k = half * 4 + kk
                tr(bk[:, kk * 128:kk * 128 + nrows], xtm[0:nrows, k * 128:(k + 1) * 128], ident[0:nrows, 0:nrows], [xtm.b, ident.b], [bk.b])
            acopy(dst[:, half * 4:half * 4 + 4, col0:col0 + nrows],
                  bk.t[:, :].rearrange("p (k c) -> p k c", k=4)[:, :, 0:nrows], [bk.b], [dstb])

    def linear_fm(wdram, kt, m_total, src, srcb, c0, n, evac, mblk=4):
        nm = (m_total + 127) // 128
        m = 0
        while m < nm:
            nblk = min(mblk, nm - m)
            ncols = min(nblk * 128, m_total - m * 128)
            wv, wvb = load_w(wdram, 0, kt * 128, m * 128, ncols)
            for mi in range(nblk):
                msz = min(128, m_total - (m + mi) * 128)
                bk = nb()
                for k in range(kt):
                    mm(bk[0:msz, 0:n], wv[:, k, mi * 128:mi * 128 + msz], src[:, k, c0:c0 + n], k == 0, k == kt - 1, [wvb, srcb], [bk.b])
                evac(m + mi, bk, msz)
            m += nblk

    def layer_norm(li, c0, n, dst, dstb):
        acopy(ubf[:, :, c0:c0 + n], u[:, :, c0:c0 + n], [u.b], [ubf.b])
        act(usq[:, :, c0:c0 + n], u[:, :, c0:c0 + n], AF.Square, [u.b], [usq.b])
        b1 = nb()
        for k in range(8):
            mm(b1[:, 0:n], onesb[:], ubf[:, k, c0:c0 + n], k == 0, k == 7, [onesb.b, ubf.b], [b1.b])
        b2 = nb()
        for k in range(8):
            mm(b2[:, 0:n], onesb[:], usq[:, k, c0:c0 + n], k == 0, k == 7, [onesb.b, usq.b], [b2.b])
        ts(st1[:, 0:n], b1[:, 0:n], 1.0 / D, ALU.mult, [b1.b], [st1.b])
        tt(st2[:, 0:n], st1[:, 0:n], st1[:, 0:n], ALU.mult, [st1.b], [st2.b])
        stt(st3[:, 0:n], b2[:, 0:n], 1.0 / D, st2[:, 0:n], ALU.mult, ALU.subtract, [b2.b, st2.b], [st3.b])
        ts(st3[:, 0:n], st3[:, 0:n], EPS, ALU.add, [st3.b], [st3.b])
        act(st3[:, 0:n], st3[:, 0:n], AF.Ln, [st3.b], [st3.b])
        act(st3[:, 0:n], st3[:, 0:n], AF.Exp, [st3.b], [st3.b], scale=-0.5)
        for k in range(8):
            tt(u[:, k, c0:c0 + n], u[:, k, c0:c0 + n], st1[:, 0:n], ALU.subtract, [u.b, st1.b], [u.b])
            tt(u[:, k, c0:c0 + n], u[:, k, c0:c0 + n], st3[:, 0:n], ALU.mult, [u.b, st3.b], [u.b])
            act(dst[:, k, c0:c0 + n], u[:, k, c0:c0 + n], AF.Identity, [u.b, lnw.b, lnb.b], [dstb],
                bias=lnb[:, li, k:k + 1], scale=lnw[:, li, k:k + 1])

    def ffn(l, c0, n):
        act_t = bigA
        def ev_gu(f0, nf, wgv, wgb, wuv, wub):
            for fi in range(nf):
                bg = nb()
                for k in range(8):
                    mm(bg[:, 0:n], wgv[:, k, fi * 128:(fi + 1) * 128], hT[:, k, c0:c0 + n], k == 0, k == 7, [wgb, hT.b], [bg.b])
                bu = nb()
                for k in range(8):
                    mm(bu[:, 0:n], wuv[:, k, fi * 128:(fi + 1) * 128], hT[:, k, c0:c0 + n], k == 0, k == 7, [wub, hT.b], [bu.b])
                act(tmpA[:, 0:n], bg[:, 0:n], AF.Silu, [bg.b], [tmpA.b])
                tt(act_t[:, f0 + fi, c0:c0 + n], tmpA[:, 0:n], bu[:, 0:n], ALU.mult, [tmpA.b, bu.b], [act_t.b])
        f = 0
        while f < 22:
            nf = min(4, 22 - f)
            wgv, wgb = load_w(w_g[l], 0, D, f * 128, nf * 128)
            wuv, wub = load_w(w_u[l], 0, D, f * 128, nf * 128)
            ev_gu(f, nf, wgv, wgb, wuv, wub)
            f += nf
        def ev_down(m, bk, msz):
            stt(u[:, m, c0:c0 + n], hT[:, m, c0:c0 + n], ALPHA, bk[:, 0:n], ALU.mult, ALU.add, [hT.b, bk.b], [u.b])
        linear_fm(w_d[l], 22, D, act_t, act_t.b, c0, n, ev_down, mblk=2)

    def linear_tm(wdram, src, srcb, c0, nrows, evac):
        for cbi in range(2):
            wv, wvb = load_w(wdram, 0, D, cbi * 512, 512)
            bk = nb()
            for k in range(8):
                mm(bk[0:nrows, :], src[:, k, c0:c0 + nrows], wv[:, k, :], k == 0, k == 7, [srcb, wvb], [bk.b])
            evac(cbi, bk)

    def linear_tm_multi(wdram, src, srcb, tl, evac, post):
        wvs = [load_w(wdram, 0, D, cbi * 512, 512) for cbi in range(2)]
        for ti, (c0, nr) in enumerate(tl):
            for cbi in range(2):
                wv, wvb = wvs[cbi]
                bk = nb()
                for k in range(8):
                    mm(bk[0:nr, :], src[:, k, c0:c0 + nr], wv[:, k, :], k == 0, k == 7, [srcb, wvb], [bk.b])
                evac(ti, cbi, bk, nr)
            post(ti, c0, nr)

    def rotary(tm, nrows, rope_src):
        dma(SP, rope_t[0:nrows, :], rope_src, [], [rope_t.b])
        v = tm.t[0:nrows, :].rearrange("r (g d) -> r g d", g=16)
        x1 = v[:, :, 0:8]
        x2 = v[:, :, 8:16]
        cos = bcast(rope_t.t, 0, [[16, nrows], [0, 16], [1, 8]])
        sin = bcast(rope_t.t, 8, [[16, nrows], [0, 16], [1, 8]])
        r = rtmp.t
        tt(r[0:nrows, :, :, 0], x1, cos, ALU.mult, [tm.b, rope_t.b], [rtmp.b])
        tt(r[0:nrows, :, :, 1], x2, sin, ALU.mult, [tm.b, rope_t.b], [rtmp.b])
        tt(r[0:nrows, :, :, 2], x2, cos, ALU.mult, [tm.b, rope_t.b], [rtmp.b])
        tt(r[0:nrows, :, :, 3], x1, sin, ALU.mult, [tm.b, rope_t.b], [rtmp.b])
        tt(x1, r[0:nrows, :, :, 0], r[0:nrows, :, :, 1], ALU.subtract, [rtmp.b], [tm.b])
        tt(x2, r[0:nrows, :, :, 2], r[0:nrows, :, :, 3], ALU.add, [rtmp.b], [tm.b])

    def ptt(out, in0, in1, op, r, w):
        S.add(POOL, lambda e: e.tensor_tensor(out=out, in0=in0, in1=in1, op=op), reads=r, writes=w)

    def ssd_sub(c0, L, pb):
        def prep_act(g):
            rb = rbc[g % 2]
            dma(SP, rb[:, :, 0:L], bass.AP(pb.t.tensor, 8 * g * TC + c0, [[0, 128], [TC, 8], [1, L]]), [pb.b], [rb.b])
            for hh in range(8):
                h = 8 * g + hh
                act(rm[0:L, hh, 0:L], rb[0:L, hh, 0:L], AF.Exp, [rb.b, nacsk.b], [rm.b], bias=nacsk[0:L, h:h + 1])
            act(rb[:, :, 0:L], rb[:, :, 0:L], AF.Exp, [rb.b], [rb.b])

        def prep_dve(g):
            rb = rbc[g % 2]
            mt_ = mts[g % 2]
            cts_ = ctss[g % 2]
            stt(mt_[0:L, :, 0:L], rm[0:L, :, 0:L], 1.0, bcast(cbm.t, g * 128, [[512, L], [0, 8], [1, L]]), ALU.min, ALU.mult,
                [rm.b, cbm.b], [mt_.b])
            tt(cts_[:, :, 0:L], rb[:, :, 0:L], bcast(CT.t, g * TC + c0, [[4 * TC, 128], [0, 8], [1, L]]), ALU.mult, [rb.b, CT.b], [cts_.b])

        def pework(g):
            mt_ = mts[g % 2]
            cts_ = ctss[g % 2]
            ybk = [nb() for _ in range(4)]
            for pr in range(4):
                for hb in range(2):
                    hh = 2 * pr + hb
                    h = 8 * g + hh
                    mm(ybk[pr][64 * hb:64 * hb + 64, 0:L], xcs[0:L, h * HP:(h + 1) * HP], mt_[0:L, hh, 0:L], True, False, [xcs.b, mt_.b], [ybk[pr].b])
            for pr in range(4):
                for hb in range(2):
                    hh = 2 * pr + hb
                    h = 8 * g + hh
                    mm(ybk[pr][64 * hb:64 * hb + 64, 0:L], sbf[:, h, :], cts_[:, hh, 0:L], False, True, [sbfg[g], cts_.b], [ybk[pr].b])
            for pr in range(4):
                j = 4 * g + pr
                bk = ybk[pr]
                tb_ = tmpBs[pr % 2]
                stt(tb_[:, 0:L], bigC[:, j, c0:c0 + L], dfm[:, j:j + 1], bk[:, 0:L], ALU.mult, ALU.add, [bigC.b, dfm.b, bk.b], [tb_.b])
                tt(bigB[:, j, c0:c0 + L], tb_[:, 0:L], bigB[:, j, c0:c0 + L], ALU.mult, [tb_.b, bigB.b], [bigB.b])
                act(ysq[:, j, c0:c0 + L], bigB[:, j, c0:c0 + L], AF.Square, [bigB.b], [ysq.b])
            bk = nb()
            mm(bk[:, :], btok[0:L, g * 128:(g + 1) * 128], xds[0:L, g * 512:(g + 1) * 512], True, True, [btok.b, xds.b], [bk.b])
            sg = s32[:, 8 * g:8 * g + 8, :]
            tt(sg, sg, bcast(etot.t, 8 * g, [[32, 128], [1, 8], [0, HP]]), ALU.mult, [s32g[g], etot.b], [s32g[g]])
            tt(sg, sg, bk.t[:, :].rearrange("p (h q) -> p h q", h=8), ALU.add, [s32g[g], bk.b], [s32g[g]])
            acopy(sbf[:, 8 * g:8 * g + 8, :], sg, [s32g[g]], [sbfg[g]])

        bk = nb()
        tr(bk[0:L, 0:32], dt_t[:, c0:c0 + L], ident[0:32, 0:32], [dt_t.b, ident.b], [bk.b])
        tr(bk[0:L, 32:64], acs_t[:, c0:c0 + L], ident[0:32, 0:32], [acs_t.b, ident.b], [bk.b])
        vcopy(dtk[0:L, :], bk[0:L, 0:32], [bk.b], [dtk.b])
        vcopy(acsk[0:L, :], bk[0:L, 32:64], [bk.b], [acsk.b])
        ts(nacsk[0:L, :], bk[0:L, 32:64], -1.0, ALU.mult, [bk.b], [nacsk.b])
        slow(SP, acsl[:], bass.AP(pb.t.tensor, c0 + L - 1, [[0, 128], [TC, 32]]), [pb.b], [acsl.b])
        prep_act(0)
        act(etot[:], acsl[:], AF.Exp, [acsl.b], [etot.b])
        tt(w2[0:L, :], acsl[0:L, :], acsk[0:L, :], ALU.subtract, [acsl.b, acsk.b], [w2.b])
        act(w2[0:L, :], w2[0:L, :], AF.Exp, [w2.b], [w2.b])
        for half in range(4):
            bk = nb()
            for jj in range(4):
                j = half * 4 + jj
                tr(psb(bk)[0:L, jj * 128:(jj + 1) * 128], bigC[:, j, c0:c0 + L], identb[:], [bigC.b, identb.b], [bk.b])
            vcopy(xcs[0:L, half * 512:(half + 1) * 512], psb(bk)[0:L, 0:512], [bk.b], [xcs.b])
        bk = nb()
        for g in range(4):
            tr(psb(bk)[0:L, g * 128:(g + 1) * 128], BT[:, g, c0:c0 + L], identb[:], [BT.b, identb.b], [bk.b])
        acopy(btok[0:L, :], psb(bk)[0:L, 0:512], [bk.b], [btok.b])
        xv = xcs.t[0:L, :].rearrange("s (h p) -> s h p", h=NH)
        tt(xv, xv, bcast(dtk.t, 0, [[32, L], [1, 32], [0, HP]]), ALU.mult, [xcs.b, dtk.b], [xcs.b])
        ptt(xds.t[0:L, :].rearrange("s (h p) -> s h p", h=NH), xv, bcast(w2.t, 0, [[32, L], [1, 32], [0, HP]]), ALU.mult,
            [xcs.b, w2.b], [xds.b])
        bk = nb()
        for g in range(4):
            mm(bk[0:L, g * 128:g * 128 + L], BT[:, g, c0:c0 + L], CT[:, g, c0:c0 + L], True, True, [BT.b, CT.b], [bk.b])
        for g in range(4):
            tt(cbm[0:L, g, 0:L], bk[0:L, g * 128:g * 128 + L], masks[0:L, 0, 0:L], ALU.mult, [bk.b, masks.b], [cbm.b])

        prep_dve(0)
        for g in range(4):
            if g + 1 < 4:
                prep_act(g + 1)
            pework(g)
            if g + 1 < 4:
                prep_dve(g + 1)

    def state_out(dst_rows, dstb):
        for q4 in range(4):
            bk = nb()
            for jj in range(4):
                j = q4 * 4 + jj
                tr(bk[:, jj * 128:(jj + 1) * 128], s32.t[:, 2 * j:2 * j + 2, :].rearrange("p h q -> p (h q)"), ident[:], [s32g[j // 4], ident.b], [bk.b])
            vcopy(stl[:, q4 * 4:q4 * 4 + 4, :], bk.t[:, :].rearrange("p (j n) -> p j n", j=4), [bk.b], [stl.b])
        dma(SP, dst_rows.rearrange("(j p) n -> p j n", p=128), stl[:], [stl.b], [dstb])

    def state_in(src_rows):
        dma(SP, stl[:], src_rows.rearrange("(j p) n -> p j n", p=128), [], [stl.b])
        for q4 in range(4):
            bk = nb()
            for jj in range(4):
                j = q4 * 4 + jj
                tr(bk[:, jj * 128:(jj + 1) * 128], stl[:, j, :], ident[:], [stl.b, ident.b], [bk.b])
            vcopy(s32.t[:, 8 * q4:8 * q4 + 8, :].rearrange("p h q -> p (h q)"), bk[:, :], [bk.b], [s32g[q4]])
            acopy(sbf[:, 8 * q4:8 * q4 + 8, :], s32[:, 8 * q4:8 * q4 + 8, :], [s32g[q4]], [sbfg[q4]])

    memset(DVE, bigA[:, :, 0:3], 0.0, [bigA.b])
    for c in range(NCHUNK):
        first = c == 0
        last = c == NCHUNK - 1
        CH = CHS[c]
        np_ = (NMETA + CH) if first else CH
        n = np_ + (NSAMP if first else 0)
        g0 = 0 if first else NMETA + XS0[c]
        pb = acs_scr[c % 2]
        col = 0
        if first:
            rows_to_fm(meta[:, :], NMETA, 0, hT, hT.b)
            col = NMETA
        for i in range(CH // 128):
            rows_to_fm(x_p[XS0[c] + 128 * i:XS0[c] + 128 * i + 128, :], 128, col + 128 * i, hT, hT.b)
        if first:
            rows_to_fm(x_s[:, :], NSAMP, np_, hT, hT.b)
            for s_ in range(NSAMP):
                dma(SP, xtm[0:72, 0:128], st_conv[3 * s_:3 * s_ + 3, :].rearrange("k (j p) -> (k j) p", p=128), [], [xtm.b])
                bk = nb()
                tr(bk[:, 0:72], xtm[0:72, 0:128], ident[0:72, 0:72], [xtm.b, ident.b], [bk.b])
                vcopy(xpre_s[:, :, s_, 0:3], bk.t[:, 0:72].rearrange("p (k j) -> p j k", k=3), [bk.b], [xpre_s.b])
            for s_ in range(NSAMP):
                dma(SP, conv_s[3 * s_:3 * s_ + 2, :], st_conv[3 * s_ + 1:3 * s_ + 3, :], [], [conv_s.b])
        def ev_in(m, bk, msz):
            if m < 16:
                act(bigB[:, m, 0:n], bk[:, 0:n], AF.Silu, [bk.b], [bigB.b])
            elif m < 40:
                j = m - 16
                vcopy(bigA[:, j, 3:3 + np_], bk[:, 0:np_], [bk.b], [bigA.b])
                if first:
                    vcopy(xpre_s[:, j, :, 3], bk[:, np_:n], [bk.b], [xpre_s.b])
                    acopy(convst[:, j, :], bk[:, np_:n], [bk.b], [convst.b])
                if last:
                    acopy(convst[:, j, 0:3], bk[:, np_ - 3:np_], [bk.b], [convst.b])
            else:
                ts(dt_t[:, 0:n], bk[0:32, 0:n], dtb[:, 0:1], ALU.add, [bk.b, dtb.b], [dt_t.b])
                act(a_t[:, 0:n], dt_t[:, 0:n], AF.Abs, [dt_t.b], [a_t.b])
                act(a_t[:, 0:n], a_t[:, 0:n], AF.Exp, [a_t.b], [a_t.b], scale=-1.0)
                act(a_t[:, 0:n], a_t[:, 0:n], AF.Ln, [a_t.b], [a_t.b], bias=1.0)
                stt(dt_t[:, 0:n], dt_t[:, 0:n], 0.0, a_t[:, 0:n], ALU.max, ALU.add, [dt_t.b, a_t.b], [dt_t.b])
                ts(a_t[:, 0:n], dt_t[:, 0:n], aneg[:, 0:1], ALU.mult, [dt_t.b, aneg.b], [a_t.b])
        linear_fm(w_in, 8, INP, hT, hT.b, 0, n, ev_in)
        if first:
            for s_ in range(NSAMP):
                slow(SP, conv_s[3 * s_ + 2, :].rearrange("(j p) -> p j", p=128), convst[:, :, s_], [convst.b], [conv_s.b])
        if last:
            for r_ in range(3):
                slow(SP, conv_p[r_, :].rearrange("(j p) -> p j", p=128), convst[:, :, r_], [convst.b], [conv_p.b])
        if not first:
            vcopy(bigA[:, :, 0:3], ctail[:], [ctail.b], [bigA.b])
        for j in range(24):
            dg = diag[j % 2]
            for k in range(4):
                ts(dg[:, k, :], identb[:], convw[:, j, k:k + 1], ALU.mult, [identb.b, convw.b], [dg.b])
            bk = nb()
            for k in range(4):
                mm(bk[:, 0:np_], dg[:, k, :], bigA[:, j, k:k + np_], k == 0, k == 3, [dg.b, bigA.b], [bk.b])
            if first:
                bk2 = nb()
                for k in range(4):
                    mm(bk2[:, 0:NSAMP], dg[:, k, :], xpre_s[:, j, :, k], k == 0, k == 3, [dg.b, xpre_s.b], [bk2.b])
            if j < 16:
                dst, dstb_, jj = bigC, bigC.b, j
            elif j < 20:
                dst, dstb_, jj = BT, BT.b, j - 16
            else:
                dst, dstb_, jj = CT, CT.b, j - 20
            act(dst[:, jj, 0:np_], bk[:, 0:np_], AF.Silu, [bk.b, convb.b], [dstb_], bias=convb[:, j:j + 1])
            if first:
                act(dst[:, jj, np_:n], bk2[:, 0:NSAMP], AF.Silu, [bk2.b, convb.b], [dstb_], bias=convb[:, j:j + 1])
        vcopy(ctail[:], bigA[:, :, np_:np_ + 3], [bigA.b], [ctail.b])
        subs = []
        if first:
            subs.append((0, NMETA))
            for i in range(CH // 128):
                subs.append((NMETA + 128 * i, 128))
            ssubs = [(np_ + s_, 1) for s_ in range(NSAMP)]
        else:
            for i in range(CH // 128):
                subs.append((128 * i, 128))
            ssubs = []
        for (c0, L) in subs + ssubs:
            S.add(DVE, lambda e, c0=c0, L=L: e.tensor_tensor_scan(out=acs_t[:, c0:c0 + L], data0=ones32[:, c0:c0 + L], data1=a_t[:, c0:c0 + L],
                                                                  initial=0.0, op0=ALU.mult, op1=ALU.add),
                  reads=[ones32.b, a_t.b], writes=[acs_t.b])
        dma(SP, pb.t[:, 0:n], acs_t[:, 0:n], [acs_t.b], [pb.b])
        if first:
            for s_ in range(NSAMP):
                state_in(st_ssm[s_])
                ssd_sub(np_ + s_, 1, pb)
                state_out(ssm_s.t[s_], ssm_s.b)
            memset(DVE, s32[:], 0.0, s32g)
            memset(DVE, sbf[:], 0.0, sbfg)
        for (c0, L) in subs:
            ssd_sub(c0, L, pb)
        if last:
            state_out(ssm_p.t[:, :], ssm_p.b)
        bk = nb()
        for j in range(16):
            mm(bk[:, 0:n], onesb[:], ysq[:, j, 0:n], j == 0, j == 15, [onesb.b, ysq.b], [bk.b])
        ts(st1[:, 0:n], bk[:, 0:n], 1.0 / DI, ALU.mult, [bk.b], [st1.b], s2=EPS, op1=ALU.add)
        act(st1[:, 0:n], st1[:, 0:n], AF.Ln, [st1.b], [st1.b])
        act(st1[:, 0:n], st1[:, 0:n], AF.Exp, [st1.b], [st1.b], scale=-0.5)
        for j in range(16):
            act(bigB[:, j, 0:n], bigB[:, j, 0:n], AF.Copy, [bigB.b, normw.b], [bigB.b], scale=normw[:, j:j + 1])
        def ev_out(m, bk, msz):
            tt(u[:, m, 0:n], bk[:, 0:n], st1[:, 0:n], ALU.mult, [bk.b, st1.b], [u.b])
            stt(u[:, m, 0:n], hT[:, m, 0:n], ALPHA, u[:, m, 0:n], ALU.mult, ALU.add, [hT.b, u.b], [u.b])
        linear_fm(w_out, 16, D, bigB, bigB.b, 0, n, ev_out, mblk=2)
        layer_norm(0, 0, n, hT, hT.b)
        ffn(0, 0, n)
        layer_norm(1, 0, n, hT, hT.b)
        tiles = []
        tiles = [(t_, min(128, np_ - t_)) for t_ in range(0, np_, 128)]
        tl = list(tiles) + ([(np_, NSAMP)] if first else [])

        def ev_kk(ti, cbi, bk, nr):
            tm_ = (ktm, vtm)[ti % 2]
            acopy(tm_[0:nr, cbi * 512:(cbi + 1) * 512], bk[0:nr, :], [bk.b], [tm_.b])

        def post_k(ti, t0, nr):
            tm_ = (ktm, vtm)[ti % 2]
            if ti >= len(tiles):
                rotary(tm_, NSAMP, rope_s[:, :])
                dma(SP, k_s[:, :], tm_[0:NSAMP, :], [tm_.b], [k_s.b])
                return
            gt = g0 + t0
            rotary(tm_, nr, rope_p[gt:gt + nr, :])
            dma(SP, k_p[gt:gt + nr, :], tm_[0:nr, :], [tm_.b], [k_p.b])
            vcopy(ktmb[0:nr, :], tm_[0:nr, :], [tm_.b], [ktmb.b])
            bk = nb()
            for h in range(8):
                tr(psb(bk)[:, h * 128:h * 128 + nr], ktmb[0:nr, h * 128:(h + 1) * 128], identb[0:nr, 0:nr], [ktmb.b, identb.b], [bk.b])
            vcopy(kTt[:, :, 0:nr], psb(bk).rearrange("p (h t) -> p h t", h=8)[:, :, 0:nr], [bk.b], [kTt.b])
            dma(SP, kT_scr.t[:, :, gt:gt + nr].rearrange("h p t -> p h t"), kTt[:, :, 0:nr], [kTt.b], [kT_scr.b])
        linear_tm_multi(w_k, hT, hT.b, tl, ev_kk, post_k)

        def ev_vv(ti, cbi, bk, nr):
            tm_ = (vtm, ktm)[ti % 2]
            acopy(tm_[0:nr, cbi * 512:(cbi + 1) * 512], bk[0:nr, :], [bk.b], [tm_.b])

        def post_v(ti, t0, nr):
            tm_ = (vtm, ktm)[ti % 2]
            if ti >= len(tiles):
                dma(SP, v_s[:, :], tm_[0:NSAMP, :], [tm_.b], [v_s.b])
                return
            gt = g0 + t0
            dma(SP, v_p[gt:gt + nr, :], tm_[0:nr, :], [tm_.b], [v_p.b])
            memset(DVE, vaug[:, :, 128:129], 1.0, [vaug.b])
            vcopy(vaug[0:nr, :, 0:128], tm_.t[0:nr, :].rearrange("r (h e) -> r h e", h=8), [tm_.b], [vaug.b])
            dma(SP, va_scr.t[gt:gt + nr, :], vaug.t[0:nr, :, :].rearrange("r h e -> r (h e)"), [vaug.b], [va_scr.b])
        linear_tm_multi(w_v, hT, hT.b, tl, ev_vv, post_v)
        if first:
            vcopy(hs1[:], hT[:, :, np_:n], [hT.b], [hs1.b])
        if stage < 2:
            continue
        n = np_
        QT = bigC
        def ev_qq(ti, cbi, bk, nr):
            tm_ = (ktm, vtm)[ti % 2]
            acopy(tm_[0:nr, cbi * 512:(cbi + 1) * 512], bk[0:nr, :], [bk.b], [tm_.b])

        def post_q(ti, t0, nr):
            tm_ = (ktm, vtm)[ti % 2]
            if ti >= len(tiles):
                rotary(tm_, NSAMP, rope_s[:, :])
                dma(SP, q_scr.t[:, :], tm_[0:NSAMP, :], [tm_.b], [q_scr.b])
                return
            gt = g0 + t0
            rotary(tm_, nr, rope_p[gt:gt + nr, :])
            vcopy(ktmb[0:nr, :], tm_[0:nr, :], [tm_.b], [ktmb.b])
            bk = nb()
            for h in range(8):
                tr(psb(bk)[:, h * 128:h * 128 + nr], ktmb[0:nr, h * 128:(h + 1) * 128], identb[0:nr, 0:nr], [ktmb.b, identb.b], [bk.b])
            vcopy(QT[:, 0:8, t0:t0 + nr], psb(bk).rearrange("p (h t) -> p h t", h=8)[:, :, 0:nr], [bk.b], [QT.b])
        linear_tm_multi(w_q, hT, hT.b, tl, ev_qq, post_q)
        gend = g0 + np_
        nkt = (gend + 127) // 128
        attnT = bigB
        nb_pool[0] = 4
        for h in range(8):
            kh = kTh[h % 2]
            vh = vah[h % 2]
            dma(SP, kh[:, 0:gend], kT_scr.t[h, :, 0:gend], [kT_scr.b], [kh.b])
            nfull = gend // 128
            rem = gend - 128 * nfull
            dma(SP, vh[:, 0:nfull, :], va_scr.t[0:nfull * 128, h * 129:(h + 1) * 129].rearrange("(t p) e -> p t e", p=128), [va_scr.b], [vh.b])
            if rem:
                dma(SP, vh[0:rem, nfull, :], va_scr.t[nfull * 128:gend, h * 129:(h + 1) * 129], [va_scr.b], [vh.b])
            for (t0, nr) in tiles:
                a = g0 + t0
                b_ = a + nr
                kts = [kt for kt in range(nkt) if 128 * kt < b_]
                ob = [banks[4 + 2 * (ob_i[0] % 2)], banks[5 + 2 * (ob_i[0] % 2)]]
                ob_i[0] += 1
                items = [(cc, kts[gi:gi + 4]) for cc in range(2) for gi in range(0, len(kts), 4)]

                def stage_a(ix):
                    cc, grp = items[ix]
                    sbk = nb()
                    for ii, kt in enumerate(grp):
                        mk = min(128, gend - 128 * kt)
                        mm(sbk[0:mk, ii * 128:ii * 128 + nr], kh[64 * cc:64 * cc + 64, kt * 128:kt * 128 + mk],
                           QT[64 * cc:64 * cc + 64, h, t0:t0 + nr], True, True, [kh.b, QT.b], [sbk.b])
                    e_ = et[et_i[0] % 3]
                    et_i[0] += 1
                    act(e_[:, 0:len(grp), 0:nr], sbk.t[:, :].rearrange("p (i q) -> p i q", i=4)[:, 0:len(grp), 0:nr], AF.Exp,
                        [sbk.b], [e_.b], scale=SCALE)
                    for ii, kt in enumerate(grp):
                        delta = a - 128 * kt
                        if delta < 127:
                            mi = {0: 0, 16: 1, -112: 2}[delta]
                            tt(e_[:, ii, 0:nr], e_[:, ii, 0:nr], masks[:, mi, 0:nr], ALU.mult, [e_.b, masks.b], [e_.b])
                    return e_

                def stage_b(ix, e_):
                    cc, grp = items[ix]
                    for ii, kt in enumerate(grp):
                        mk = min(128, gend - 128 * kt)
                        mm(ob[cc][0:nr, 0:129], e_[0:mk, ii, 0:nr], vh[0:mk, kt, :], kt == kts[0], kt == kts[-1], [e_.b, vh.b], [ob[cc].b])

                pend = stage_a(0)
                steps = []
                if pend_tr[0] is not None:
                    steps = pend_tr[0]()
                    pend_tr[0] = None
                for ix in range(len(items)):
                    nxt = stage_a(ix + 1) if ix + 1 < len(items) else None
                    stage_b(ix, pend)
                    pend = nxt
                    if steps and ix < 3:
                        steps.pop(0)()
                for st_ in steps:
                    st_()
                def _tail_steps(ob=ob, h=h, t0=t0, nr=nr):
                    ofb = ofbs[ofb_i[0] % 2]
                    ofb_i[0] += 1

                    def s_dve():
                        S.add(DVE, lambda e: e.reciprocal(out=sm[0:nr, 0:1], in_=ob[0][0:nr, 128:129]), reads=[ob[0].b], writes=[sm.b])
                        S.add(DVE, lambda e: e.reciprocal(out=sm[0:nr, 1:2], in_=ob[1][0:nr, 128:129]), reads=[ob[1].b], writes=[sm.b])
                        tt(sm[0:nr, 1:2], sm[0:nr, 1:2], lam_v[0:nr, 3:4], ALU.mult, [sm.b, lam_v.b], [sm.b])
                        ts(ofin[0:nr, :], ob[0][0:nr, 0:128], sm[0:nr, 0:1], ALU.mult, [ob[0].b, sm.b], [ofin.b])
                        stt(ofin[0:nr, :], ob[1][0:nr, 0:128], sm[0:nr, 1:2], ofin[0:nr, :], ALU.mult, ALU.add, [ob[1].b, sm.b, ofin.b], [ofin.b])

                    def s_act():
                        act(junk[0:nr, :], ofin[0:nr, :], AF.Square, [ofin.b], [junk.b, sm.b], accum=sm[0:nr, 2:3])
                        act(sm[0:nr, 2:3], sm[0:nr, 2:3], AF.Ln, [sm.b, epsc.b], [sm.b], scale=1.0 / 128, bias=epsc[0:nr, 0:1])
                        act(sm[0:nr, 2:3], sm[0:nr, 2:3], AF.Exp, [sm.b], [sm.b], scale=-0.5)

                    def s_fin():
                        stt(ofb[0:nr, :], ofin[0:nr, :], sm[0:nr, 2:3], subw_t[0:nr, :], ALU.mult, ALU.mult, [ofin.b, sm.b, subw_t.b], [ofb.b])

                    def s_tr():
                        tb = nb()
                        tr(psb(tb)[:, 0:nr], ofb[0:nr, :], identb[0:nr, 0:nr], [ofb.b, identb.b], [tb.b])
                        vcopy(attnT[:, h, t0:t0 + nr], psb(tb)[:, 0:nr], [tb.b], [attnT.b])
                    return [s_dve, s_act, s_fin, s_tr]
                pend_tr[0] = _tail_steps
        if pend_tr[0] is not None:
            for st_ in pend_tr[0]():
                st_()
            pend_tr[0] = None
        nb_pool[0] = 8

        def ev_o(m, bk, msz):
            stt(u[:, m, 0:n], hT[:, m, 0:n], ALPHA, bk[:, 0:n], ALU.mult, ALU.add, [hT.b, bk.b], [u.b])
        linear_fm(w_o, 8, D, attnT, attnT.b, 0, n, ev_o)
        layer_norm(2, 0, n, hT, hT.b)
        ffn(1, 0, n)
        layer_norm(3, 0, n, yfm, yfm.b)
        for (t0, nr) in tiles:
            for half in range(2):
                bk = nb()
                for kk in range(4):
                    k = half * 4 + kk
                    tr(bk[0:nr, kk * 128:(kk + 1) * 128], yfm[:, k, t0:t0 + nr], ident[:], [yfm.b, ident.b], [bk.b])
                acopy(ytm[0:nr, half * 512:(half + 1) * 512], bk[0:nr, :], [bk.b], [ytm.b])
            gt = g0 + t0
            lo = max(gt, NMETA)
            hi = gt + nr
            if hi > lo:
                dma(SP, y_p[lo - NMETA:hi - NMETA, :], ytm[lo - gt:nr, :], [ytm.b], [y_p.b])


    if stage >= 3:
        flatA = bigA.t[:, :, :].rearrange("p a b -> p (a b)")
        flatB = bigB.t[:, :, :].rearrange("p a b -> p (a b)")
        flatC = bigC.t[:, :, :].rearrange("p a b -> p (a b)")
        W4 = TT_ * D
        KTs = [T(flatA[:, i * W4:(i + 1) * W4].rearrange("p (t d) -> p t d", t=TT_), S.buf(f"Kt{i}")) for i in range(2)]
        VTs = [T(flatB[:, 0:W4].rearrange("p (t d) -> p t d", t=TT_), S.buf("Vt0")),
               T(flatC[:, 0:W4].rearrange("p (t d) -> p t d", t=TT_), S.buf("Vt1"))]
        PRs = [T(wbufs[i].t[:, 0:W4].rearrange("p (t d) -> p t d", t=TT_), S.buf(f"prod{i}")) for i in range(2)]
        parents = {id(KTs[0]): [bigA.b], id(KTs[1]): [bigA.b], id(VTs[0]): [bigB.b], id(VTs[1]): [bigC.b],
                   id(PRs[0]): [wbufs[0].b], id(PRs[1]): [wbufs[1].b]}
        used = set()

        def wv(view):
            if id(view) in used:
                return [view.b]
            used.add(id(view))
            return [view.b] + parents[id(view)]
        esum = sb("esum", [128, 16], F32)
        esumb = sb("esumb", [128, 8, 4], F32)
        etile = sb("etile", [128, 16], F32)
        ones1 = sb("ones1", [128, 1], F32)
        den = sb("den", [4, 8], F32)
        idxt = sb("idxt", [128, 1], I32)
        idx2 = sb("idx2", [128, NTT], I32)
        qbc = T(xcs.t[:, 0:D], xcs.b)
        selp = sb("selp_sb", [128, 2], F32)
        cmask = sb("cmask_sb", [4, 2], F32)
        combc = sb("combc", [4, 2, 2], F32)
        comb = sb("comb_sb", [4, 2], F32)
        scs = [sb(f"sc{i}", [128, TT_, 16], F32) for i in range(2)]
        eblks = [sb(f"eblk{i}", [128, TT_, 8, 4], BF16) for i in range(2)]
        oacc = T(xds.t[:, :].bitcast(F32)[0:4, :].rearrange("m (h e) -> m h e", h=8), xds.b)
        uflat = u.t[:, :, :].rearrange("p a b -> p (a b)")
        pn = T(uflat[0:4, 0:1024].rearrange("m (g d) -> m g d", g=16), u.b)
        sn = sb("sn", [4, 16], F32)
        sn2 = sb("sn2", [4, 8], F32)
        on = T(vtm.t[0:4, :], vtm.b)
        o2 = T(uflat[0:2, 0:1024], u.b)
        o2s = T(uflat[0:2, 1024:2048], u.b)
        r2 = sb("r2", [2, 8], F32)
        attnT_s = sb("attnT_s", [128, 8, NSAMP], BF16)
        dma(SP, selp[:], selp_in, [], [selp.b])
        dma(SP, cmask[:], cmask_in, [], [cmask.b])
        dma(SP, combc[:], comb_in.rearrange("a m b -> m a b"), [], [combc.b])
        stt(comb[:], combc[:, 1, :], lam_v[0:4, 3:4], combc[:, 0, :], ALU.mult, ALU.add, [combc.b, lam_v.b], [comb.b])
        memset(DVE, ones1[:], 1.0, [ones1.b])
        for pr in range(2):
            dma(SP, idxt[:], ptab[128 * pr:128 * pr + 128, :], [], [idxt.b])
            for ti in range(NTT):
                ts(idx2[:, ti:ti + 1], idxt[:, 0:1], float(NTT), ALU.mult, [idxt.b], [idx2.b], s2=float(ti), op1=ALU.add)
            for b2 in range(2):
                dma(POOL, qbc[64 * b2:64 * b2 + 64, :], bass.AP(q_scr.t.tensor, (2 * pr + b2) * D, [[0, 64], [1, D]]), [q_scr.b], [qbc.b])
            memset(DVE, oacc[:], 0.0, [oacc.b])
            memset(DVE, esum[:], 0.0, [esum.b])
            for ti in range(NTT):
                Kt, Vt, prod, sc, eblk = KTs[ti % 2], VTs[ti % 2], PRs[ti % 2], scs[ti % 2], eblks[ti % 2]
                S.add(POOL, lambda e, ti=ti, Kt=Kt: e.indirect_dma_start(out=Kt.t.rearrange("p t d -> p (t d)"), out_offset=None, in_=cache_k[:, :],
                                                                          in_offset=bass.IndirectOffsetOnAxis(ap=idx2[:, ti:ti + 1], axis=0)),
                      reads=[idx2.b], writes=wv(Kt), dma=True)
                S.add(POOL, lambda e, ti=ti, Vt=Vt: e.indirect_dma_start(out=Vt.t.rearrange("p t d -> p (t d)"), out_offset=None, in_=cache_v[:, :],
                                                                          in_offset=bass.IndirectOffsetOnAxis(ap=idx2[:, ti:ti + 1], axis=0)),
                      reads=[idx2.b], writes=wv(Vt), dma=True)
                tt(prod[:, :, :], Kt[:, :, :], bcast(xcs.t, 0, [[DI, 128], [0, TT_], [1, D]]), ALU.mult, [Kt.b, qbc.b], wv(prod))
                S.add(DVE, lambda e, prod=prod, sc=sc: e.tensor_reduce(out=sc[:, :, :], in_=prod.t.rearrange("p t (g d) -> p t g d", g=16), axis=AX.X, op=ALU.add),
                      reads=[prod.b], writes=[sc.b])
                act(sc[:, :, :], sc[:, :, :], AF.Exp, [sc.b], [sc.b], scale=SCALE)
                S.add(DVE, lambda e, sc=sc: e.tensor_reduce(out=etile[:, :], in_=sc.t[:, :, :].rearrange("p t g -> p g t"), axis=AX.X, op=ALU.add),
                      reads=[sc.b], writes=[etile.b])
                tt(esum[:, :], esum[:, :], etile[:, :], ALU.add, [esum.b, etile.b], [esum.b])
                scv = sc.t[:, :, :].rearrange("p t (h c) -> p t h c", h=8)
                for b2 in range(2):
                    ts(eblk.t[:, :, :, :].rearrange("p t h (c b) -> p t h c b", c=2)[:, :, :, :, b2],
                       scv, selp[:, b2:b2 + 1], ALU.mult, [sc.b, selp.b], [eblk.b])
                pbk = [nb(), nb()]
                for h in range(8):
                    o = pbk[h // 4][0:4, (h % 4) * 128:(h % 4) * 128 + 128]
                    for t in range(TT_):
                        mm(o, eblk[:, t, h, :], Vt[:, t, h * 128:(h + 1) * 128], t == 0, t == TT_ - 1, [eblk.b, Vt.b], [pbk[h // 4].b])
                for bi in range(2):
                    ov = oacc.t[:, 4 * bi:4 * bi + 4, :].rearrange("m h e -> m (h e)")
                    tt(ov, ov, pbk[bi][0:4, :], ALU.add, [oacc.b, pbk[bi].b], [oacc.b])
            esv = esum.t[:, :].rearrange("p (h c) -> p h c", h=8)
            for b2 in range(2):
                ts(esumb.t[:, :, :].rearrange("p h (c b) -> p h c b", c=2)[:, :, :, b2], esv, selp[:, b2:b2 + 1], ALU.mult, [esum.b, selp.b], [esumb.b])
            bk = nb()
            for h in range(8):
                mm(bk[0:4, h:h + 1], esumb[:, h, :], ones1[:, :], True, True, [esumb.b, ones1.b], [bk.b])
            vcopy(den[:, :], bk[0:4, 0:8], [bk.b], [den.b])
            for cc in range(2):
                dma(SP, ktm[2 * cc:2 * cc + 2, :], q_scr.t[2 * pr:2 * pr + 2, :], [q_scr.b], [ktm.b])
                dma(SP, vtm[2 * cc:2 * cc + 2, :], k_s.t[2 * pr:2 * pr + 2, :], [k_s.b], [vtm.b])
            tt(pn[:, :, :], ktm.t[0:4, :].rearrange("m (g d) -> m g d", g=16), vtm.t[0:4, :].rearrange("m (g d) -> m g d", g=16), ALU.mult, [ktm.b, vtm.b], [pn.b])
            for cc in range(2):
                dma(SP, ktm[2 * cc:2 * cc + 2, :], v_s.t[2 * pr:2 * pr + 2, :], [v_s.b], [ktm.b])
            S.add(DVE, lambda e: e.tensor_reduce(out=sn[:, :], in_=pn[:, :, :], axis=AX.X, op=ALU.add), reads=[pn.b], writes=[sn.b])
            act(sn[:, :], sn[:, :], AF.Exp, [sn.b], [sn.b], scale=SCALE)
            snv = sn.t[:, :].rearrange("m (h c) -> m h c", h=8)
            tt(snv, snv, bcast(cmask.t, 0, [[2, 4], [0, 8], [1, 2]]), ALU.mult, [sn.b, cmask.b], [sn.b])
            S.add(DVE, lambda e: e.tensor_reduce(out=sn2[:, :], in_=sn.t[:, :].rearrange("m (h c) -> m h c", h=8), axis=AX.X, op=ALU.add), reads=[sn.b], writes=[sn2.b])
            tt(on.t[:, :].rearrange("m (h e) -> m h e", h=8), ktm.t[0:4, :].rearrange("m (h e) -> m h e", h=8),
               bcast(sn2.t, 0, [[8, 4], [1, 8], [0, 128]]), ALU.mult, [ktm.b, sn2.b], [on.b])
            tt(oacc[:, :, 0:128], oacc[:, :, 0:128], on.t[:, :].rearrange("m (h e) -> m h e", h=8), ALU.add, [oacc.b, on.b], [oacc.b])
            tt(den[:, :], den[:, :], sn2[:, :], ALU.add, [den.b, sn2.b], [den.b])
            S.add(DVE, lambda e: e.reciprocal(out=sn2[:, :], in_=den[:, :]), reads=[den.b], writes=[sn2.b])
            tt(on.t[:, :].rearrange("m (h e) -> m h e", h=8), oacc[:, :, 0:128], bcast(sn2.t, 0, [[8, 4], [1, 8], [0, 128]]), ALU.mult,
               [oacc.b, sn2.b], [on.b])
            for half in range(2):
                bk = nb()
                mm(bk[0:2, :], comb[:, :], on[:, half * 512:(half + 1) * 512], True, True, [comb.b, on.b], [bk.b])
                vcopy(o2[:, half * 512:(half + 1) * 512], bk[0:2, :], [bk.b], [o2.b])
            tt(o2s[:, :], o2[:, :], o2[:, :], ALU.mult, [o2.b], [o2s.b])
            S.add(DVE, lambda e: e.tensor_reduce(out=r2[:, :], in_=o2s.t[:, :].rearrange("b (h e) -> b h e", h=8), axis=AX.X, op=ALU.add), reads=[o2s.b], writes=[r2.b])
            ts(r2[:, :], r2[:, :], 1.0 / 128, ALU.mult, [r2.b], [r2.b], s2=EPS, op1=ALU.add)
            act(r2[:, :], r2[:, :], AF.Ln, [r2.b], [r2.b])
            act(r2[:, :], r2[:, :], AF.Exp, [r2.b], [r2.b], scale=-0.5)
            o2v = o2.t[:, :].rearrange("b (h e) -> b h e", h=8)
            tt(o2v, o2v, bcast(r2.t, 0, [[8, 2], [1, 8], [0, 128]]), ALU.mult, [o2.b, r2.b], [o2.b])
            tt(o2v, o2v, bcast(subw_t.t, 0, [[128, 2], [0, 8], [1, 128]]), ALU.mult, [o2.b, subw_t.b], [o2.b])
            bk = nb()
            for h in range(8):
                tr(bk[:, 2 * h:2 * h + 2], o2[0:2, h * 128:(h + 1) * 128], ident[0:2, 0:2], [o2.b, ident.b], [bk.b])
            vcopy(attnT_s[:, :, 2 * pr:2 * pr + 2], bk.t[:, 0:16].rearrange("p (h b) -> p h b", h=8), [bk.b], [attnT_s.b])
        S.add(DVE, lambda e: e.memset(ones1[:], 1.0), reads=[v.b for v in KTs + VTs + PRs],
              writes=[ones1.b, bigA.b, bigB.b, bigC.b, wbufs[0].b, wbufs[1].b])
        n = NSAMP
        vcopy(hT[:, :, 0:n], hs1[:], [hs1.b], [hT.b])
        def ev_os(m, bk, msz):
            stt(u[:, m, 0:n], hT[:, m, 0:n], ALPHA, bk[:, 0:n], ALU.mult, ALU.add, [hT.b, bk.b], [u.b])
        linear_fm(w_o, 8, D, attnT_s, attnT_s.b, 0, n, ev_os)
        layer_norm(2, 0, n, hT, hT.b)
        ffn(1, 0, n)
        layer_norm(3, 0, n, yfm, yfm.b)
        for half in range(2):
            bk = nb()
            for kk in range(4):
                k = half * 4 + kk
                tr(bk[0:n, kk * 128:(kk + 1) * 128], yfm[:, k, 0:n], ident[:], [yfm.b, ident.b], [bk.b])
            acopy(ytm[0:n, half * 512:(half + 1) * 512], bk[0:n, :], [bk.b], [ytm.b])
        dma(SP, y_s[:, :], ytm[0:n, :], [ytm.b], [y_s.b])

    flush_wr()
    finals = [k_p.b, v_p.b, ssm_p.b, conv_p.b, k_s.b, v_s.b, ssm_s.b, conv_s.b, y_p.b, y_s.b]
    S.emit(final_bufs=finals)
    st.close()
    return nc


def _host_consts():
    half = 8
    inv = (500000.0 ** (-(np.arange(half, dtype=np.float32) * 2.0 / 16))).astype(np.float32)
    def tab(pos):
        ang = pos.astype(np.float32)[:, None] * inv[None, :]
        return np.concatenate([np.cos(ang), np.sin(ang)], 1).astype(np.float32)
    rope_p = tab(np.arange(LTOT))
    rope_s = tab(np.full((NSAMP,), PAST))
    r = np.arange(128)[:, None]
    j = np.arange(128)[None, :]
    masks = np.stack([(r - j <= 0), (r - j <= 16), (r - j <= -112)]).astype(np.float32)
    selq = np.zeros((2, NSAMP, 128), np.float32)
    rowsel = np.zeros((2, NSAMP, 4), np.float32)
    for pr in range(2):
        for p in range(128):
            selq[pr, 2 * pr + p // 64, p] = 1.0
        for m_ in range(4):
            rowsel[pr, 2 * pr + (m_ % 2), m_] = 1.0
    selp = np.zeros((128, 2), np.float32)
    selp[:64, 0] = 1.0
    selp[64:, 1] = 1.0
    cmask = np.zeros((4, 2), np.float32)
    comb = np.zeros((2, 4, 2), np.float32)
    for m_ in range(4):
        cmask[m_, m_ // 2] = 1.0
        comb[m_ // 2, m_, m_ % 2] = 1.0
    return rope_p, rope_s, masks, dict(selp=selp, cmask=cmask, comb=comb)


def kernel(x_prompt, x_sample, cache_k, cache_v, state_ssm, state_conv, page_table, meta_tokens,
           a_w_in, a_conv_w, a_conv_b, a_dt_bias, a_A_log, a_D, a_norm_w, a_w_out,
           kv_w_k, kv_w_v, b_w_q, b_lambda, b_subln_w, b_w_o,
           ffn_w_gate, ffn_w_up, ffn_w_down, ln_mix_w, ln_mix_b, ln_ffn_w, ln_ffn_b, _stage=99):
    f = lambda a: np.ascontiguousarray(np.asarray(a))
    rope_p, rope_s, masks, dconst = _host_consts()
    percore = isinstance(cache_k, list)
    npool = int(np.asarray(cache_k[0] if percore else cache_k).shape[0])
    nc = build_program(_stage, npool)
    ck = cv = None
    if _stage >= 3 and not percore:
        ck = f(cache_k).reshape(npool * 32, 4 * D)
        cv = f(cache_v).reshape(npool * 32, 4 * D)
    ln_w = np.stack([f(ln_mix_w)[0], f(ln_ffn_w)[0], f(ln_mix_w)[1], f(ln_ffn_w)[1]]).reshape(4, D, 1)
    ln_b = np.stack([f(ln_mix_b)[0], f(ln_ffn_b)[0], f(ln_mix_b)[1], f(ln_ffn_b)[1]]).reshape(4, D, 1)
    shared = dict(
        meta=f(meta_tokens),
        w_in=f(a_w_in)[0], conv_w=f(a_conv_w)[0], conv_b=f(a_conv_b)[0].reshape(CONVD, 1),
        dt_bias=f(a_dt_bias)[0].reshape(NH, 1), a_log=f(a_A_log)[0].reshape(NH, 1), d_skip=f(a_D)[0].reshape(NH, 1),
        norm_w=f(a_norm_w)[0].reshape(DI, 1), w_out=f(a_w_out)[0], w_k=f(kv_w_k), w_v=f(kv_w_v), w_q=f(b_w_q)[0],
        lam=f(b_lambda)[0].reshape(1, 256), subw=f(b_subln_w)[0].reshape(1, 128), w_o=f(b_w_o)[0],
        w_g=f(ffn_w_gate), w_u=f(ffn_w_up), w_d=f(ffn_w_down), ln_w=ln_w, ln_b=ln_b,
        rope_p=rope_p, rope_s=rope_s, masks=masks,
    )
    if _stage >= 3:
        shared.update(dconst)
        if not percore:
            shared["cache_k"] = ck
            shared["cache_v"] = cv
    xp = f(x_prompt)
    xs = f(x_sample).reshape(32, D)
    ss = f(state_ssm).reshape(32, NH * HP, NS)
    sc = f(state_conv).reshape(32, 3, CONVD)
    pt = f(page_table).astype(np.int32) if not percore else None
    in_maps = []
    for c in range(NCORES):
        m = dict(shared)
        m["x_p"] = xp[c]
        m["x_s"] = xs[4 * c:4 * c + 4]
        m["st_ssm"] = ss[4 * c:4 * c + 4]
        m["st_conv"] = sc[4 * c:4 * c + 4].reshape(12, CONVD)
        if _stage >= 3 and not percore:
            m["ptab"] = pt[4 * c:4 * c + 4].reshape(NSAMP * NPAGES, 1)
        if _stage >= 3 and percore:
            m["ptab"] = np.asarray(page_table[c], np.int32).reshape(NSAMP * NPAGES, 1)
            m["cache_k"] = f(cache_k[c]).reshape(npool * 32, 4 * D)
            m["cache_v"] = f(cache_v[c]).reshape(npool * 32, 4 * D)
        in_maps.append(m)
    res = run_bass_kernel_spmd(nc, in_maps, core_ids=list(range(NCORES))).results
    cat = lambda k: np.stack([r[k] for r in res])
    y_prompt = cat("y_p").reshape(8, SEQ, D)
    y_sample = cat("y_s").reshape(32, 1, D)
    k_prompt = cat("k_p").reshape(8, LTOT, 8, 2, 64)
    v_prompt = cat("v_p").reshape(8, LTOT, 8, 128)
    ssm_prompt = cat("ssm_p").reshape(1, 8, NH, HP, NS)
    conv_prompt = cat("conv_p").reshape(1, 8, 3, CONVD)
    k_sample = cat("k_s").reshape(32, 1, 8, 2, 64)
    v_sample = cat("v_s").reshape(32, 1, 8, 128)
    ssm_sample = cat("ssm_s").reshape(1, 32, NH, HP, NS)
    conv_sample = cat("conv_s").reshape(1, 32, 3, CONVD)
    return (y_prompt, y_sample, k_prompt, v_prompt, ssm_prompt, conv_prompt, k_sample, v_sample, ssm_sample, conv_sample)
```

```python
import contextlib
import math
import numpy as np
import concourse.bass as bass
import concourse.mybir as mybir
from concourse.bass_utils import run_bass_kernel_spmd

F32 = mybir.dt.float32
BF16 = mybir.dt.bfloat16
I32 = mybir.dt.int32
AF = mybir.ActivationFunctionType
ALU = mybir.AluOpType
AX = mybir.AxisListType

PE, ACT, DVE, POOL, SP = "pe", "act", "dve", "pool", "sp"
ENGS = [PE, ACT, DVE, POOL, SP]
N_DMA_SLOTS = 12

NCORES = 8
D = 1024
SEQ = 2048
NMETA = 16
LTOT = SEQ + NMETA
DI = 2048
NH = 32
HP = 64
NG = 4
NS = 128
CONVD = 3072
INP = 5152
DFF = 2816
NSAMP = 4
PAST = 8192
NPAGES = 64
NPOOL = 2560
ALPHA = 4 ** 0.25
EPS = 1e-5
LAMBDA_INIT = 0.8 - 0.6 * math.exp(-0.3 * 1)
CHS = [384, 384, 384, 384, 384, 128]
NCHUNK = len(CHS)
XS0 = [sum(CHS[:i]) for i in range(NCHUNK)]
TC = NMETA + max(CHS) + NSAMP
SCALE = 64 ** -0.5


class Buf:
    __slots__ = ("name", "w", "rs")

    def __init__(self, name):
        self.name = name
        self.w = None
        self.rs = []


class Sched:
    def __init__(self, nc):
        self.nc = nc
        self.recs = {e: [] for e in ENGS}
        self.waited = {e: {} for e in ENGS}
        self.dma_slot_next = {e: 0 for e in ENGS}
        self.dma_slot_uses = {}
        self.nbuf = 0

    def buf(self, name=None):
        self.nbuf += 1
        return Buf(name or f"b{self.nbuf}")

    def _need(self, eng, tok, waits):
        if tok is None:
            return
        if tok[0] == "c":
            _, e2, i2 = tok
            if e2 == PE and eng == PE:
                return
            key = ("c", e2)
            val = i2
        else:
            _, e2, slot, use = tok
            key = ("d", e2, slot)
            val = use
        if self.waited[eng].get(key, -1) >= val:
            return
        for j, (k, v, t) in enumerate(waits):
            if k == key:
                if v < val:
                    waits[j] = (key, val, tok)
                return
        waits.append((key, val, tok))

    def add(self, eng, fn, reads=(), writes=(), dma=False):
        waits = []
        for b in reads:
            self._need(eng, b.w, waits)
        for b in writes:
            self._need(eng, b.w, waits)
            for r in b.rs:
                self._need(eng, r, waits)
        idx = len(self.recs[eng])
        if dma:
            slot = self.dma_slot_next[eng]
            self.dma_slot_next[eng] = (slot + 1) % N_DMA_SLOTS
            use = self.dma_slot_uses.get((eng, slot), 0) + 1
            self.dma_slot_uses[(eng, slot)] = use
            if use > 1:
                self._need(eng, ("d", eng, slot, use - 1), waits)
            tok = ("d", eng, slot, use)
        else:
            tok = ("c", eng, idx)
        for (key, val, t) in waits:
            self.waited[eng][key] = val
        self.recs[eng].append(dict(fn=fn, waits=[t for (_, _, t) in waits], tok=tok, sig=False))
        for b in reads:
            b.rs.append(tok)
        for b in writes:
            b.w = tok
            b.rs = []
        return tok

    def emit(self, final_bufs=()):
        nc = self.nc
        fin_waits = []
        for b in final_bufs:
            self._need(SP, b.w, fin_waits)
        fin = [t for (_, _, t) in fin_waits]
        for e in ENGS:
            for rec in self.recs[e]:
                for t in rec["waits"]:
                    if t[0] == "c":
                        self.recs[t[1]][t[2]]["sig"] = True
        for t in fin:
            if t[0] == "c":
                self.recs[t[1]][t[2]]["sig"] = True
        cnt = {}
        for e in ENGS:
            c = 0
            for i, rec in enumerate(self.recs[e]):
                if rec["tok"][0] == "c" and rec["sig"]:
                    c += 1
                cnt[(e, i)] = c
        with contextlib.ExitStack() as st:
            csem = {e: st.enter_context(nc.semaphore(f"cs_{e}")) for e in ENGS}
            dsem = {}
            for e in ENGS:
                for s in range(N_DMA_SLOTS):
                    if (e, s) in self.dma_slot_uses:
                        dsem[(e, s)] = st.enter_context(nc.semaphore(f"ds_{e}{s}"))
            block = st.enter_context(nc.Block())

            def wait_tok(engine, t):
                if t[0] == "c":
                    engine.wait_ge(csem[t[1]], cnt[(t[1], t[2])])
                else:
                    engine.wait_ge(dsem[(t[1], t[2])], 16 * t[3])

            def run(ename, engine, extra_final=None):
                for rec in self.recs[ename]:
                    for t in rec["waits"]:
                        wait_tok(engine, t)
                    ins = rec["fn"](engine)
                    tok = rec["tok"]
                    if tok[0] == "d":
                        ins.then_inc(dsem[(tok[1], tok[2])], 16)
                    elif rec["sig"]:
                        ins.then_inc(csem[ename], 1)
                if extra_final:
                    for t in extra_final:
                        wait_tok(engine, t)

            @block.tensor
            def _(eng):
                run(PE, eng)

            @block.scalar
            def _(eng):
                run(ACT, eng)

            @block.vector
            def _(eng):
                run(DVE, eng)

            @block.gpsimd
            def _(eng):
                run(POOL, eng)

            @block.sync
            def _(eng):
                run(SP, eng, fin)


class T:
    def __init__(self, t, b):
        self.t = t
        self.b = b

    def __getitem__(self, k):
        return self.t[k]


def build_program(stage=99, npool=NPOOL):
    nc = bass.Bass("TRN2", target_bir_lowering=False)
    S = Sched(nc)
    st = contextlib.ExitStack()

    def din(name, shape, dt=F32):
        return nc.dram_tensor(name, list(shape), dt, kind="ExternalInput").ap()

    def dout(name, shape, dt=F32):
        return T(nc.dram_tensor(name, list(shape), dt, kind="ExternalOutput").ap(), S.buf(name))

    def dscr(name, shape, dt):
        return T(nc.dram_tensor(name, list(shape), dt, kind="Internal").ap(), S.buf(name))

    def sb(name, shape, dt=F32):
        return T(st.enter_context(nc.sbuf_tensor(name, list(shape), dt)), S.buf(name))

    x_p = din("x_p", [SEQ, D])
    x_s = din("x_s", [NSAMP, D])
    meta = din("meta", [NMETA, D])
    if stage >= 3:
        TT_ = 4
        NTT = 128 // TT_
        cache_k = din("cache_k", [npool * NTT, TT_ * D])
        cache_v = din("cache_v", [npool * NTT, TT_ * D])
        ptab = din("ptab", [NSAMP * NPAGES, 1], I32)
        selp_in = din("selp", [128, 2])
        cmask_in = din("cmask", [4, 2])
        comb_in = din("comb", [2, 4, 2])
    st_ssm = din("st_ssm", [NSAMP, NH * HP, NS])
    st_conv = din("st_conv", [NSAMP * 3, CONVD])
    w_in = din("w_in", [D, INP])
    conv_w = din("conv_w", [4, CONVD])
    conv_b = din("conv_b", [CONVD, 1])
    dt_bias = din("dt_bias", [NH, 1])
    a_log = din("a_log", [NH, 1])
    d_skip = din("d_skip", [NH, 1])
    norm_w = din("norm_w", [DI, 1])
    w_out = din("w_out", [DI, D])
    w_k = din("w_k", [D, D])
    w_v = din("w_v", [D, D])
    w_q = din("w_q", [D, D])
    lam = din("lam", [1, 256])
    subw = din("subw", [1, 128])
    w_o = din("w_o", [D, D])
    w_g = din("w_g", [2, D, DFF])
    w_u = din("w_u", [2, D, DFF])
    w_d = din("w_d", [2, DFF, D])
    ln_w = din("ln_w", [4, D, 1])
    ln_b = din("ln_b", [4, D, 1])
    rope_p = din("rope_p", [LTOT, 16])
    rope_s = din("rope_s", [NSAMP, 16])
    masks_in = din("masks", [3, 128, 128])

    y_p = dout("y_p", [SEQ, D])
    y_s = dout("y_s", [NSAMP, D])
    k_p = dout("k_p", [LTOT, D])
    v_p = dout("v_p", [LTOT, D])
    ssm_p = dout("ssm_p", [NH * HP, NS])
    conv_p = dout("conv_p", [3, CONVD])
    k_s = dout("k_s", [NSAMP, D])
    v_s = dout("v_s", [NSAMP, D])
    ssm_s = dout("ssm_s", [NSAMP, NH * HP, NS])
    conv_s = dout("conv_s", [NSAMP * 3, CONVD])

    kT_scr = dscr("kT_scr", [8, 128, LTOT + 112], BF16)
    va_scr = dscr("va_scr", [LTOT + 112, 8 * 129], BF16)
    q_scr = dscr("q_scr", [NSAMP, D], F32)
    acs_scr = [dscr(f"acs_scr{i}", [NH, TC], F32) for i in range(2)]

    banks = []
    for i in range(8):
        t = st.enter_context(nc.psum_tensor(f"bank{i}", [128, 512], F32))
        banks.append(T(t, S.buf(f"bank{i}")))
    bank_i = [0]
    nb_pool = [8]

    def nb():
        b = banks[bank_i[0] % nb_pool[0]]
        bank_i[0] += 1
        return b

    ident = sb("ident", [128, 128], F32)
    identb = sb("identb", [128, 128], BF16)
    onesb = sb("onesb", [128, 128], BF16)
    masks = sb("masks_sb", [128, 3, 128], BF16)
    hT = sb("hT", [128, 8, TC], BF16)
    u = sb("u", [128, 8, TC], F32)
    bigA = sb("bigA", [128, 24, TC + 3], BF16)
    bigB = sb("bigB", [128, 16, TC], BF16)
    bigC = sb("bigC", [128, 16, TC], BF16)
    ysq = T(bigA.t[:, 0:16, 0:TC], bigA.b)
    ubf = T(bigA.t[:, 0:8, 0:TC], bigA.b)
    usq = T(bigA.t[:, 8:16, 0:TC], bigA.b)
    BT = sb("BT", [128, 4, TC], BF16)
    CT = sb("CT", [128, 4, TC], BF16)
    NWB = 3
    WCOLS = 5632
    wbufs = [sb(f"wbuf{i}", [128, WCOLS], BF16) for i in range(NWB)]
    wb_i = [0]
    st1 = sb("st1", [128, TC], F32)
    st3 = sb("st3", [128, TC], F32)
    tmpA = sb("tmpA", [128, 512], F32)
    st2 = T(tmpA.t[:, 0:TC], tmpA.b)
    tmpBs = [sb(f"tmpB{i}", [128, 512], F32) for i in range(2)]
    ktm = sb("ktm", [128, D], F32)
    xtm = ktm
    ktmb = sb("ktmb", [128, D], BF16)
    vtm = sb("vtm", [128, D], F32)
    vaug = sb("vaug", [128, 8, 129], BF16)
    kTt = sb("kTt", [128, 8, 128], BF16)
    rope_t = sb("rope_t", [128, 16], F32)
    rtmp = sb("rtmp", [128, 16, 8, 4], F32)
    lnw = sb("lnw", [128, 4, 8], F32)
    lnb = sb("lnb", [128, 4, 8], F32)
    convw = sb("convw", [128, 24, 4], F32)
    convb = sb("convb", [128, 24], F32)
    normw = sb("normw", [128, 16], F32)
    dfm = sb("dfm", [128, 16], F32)
    dtb = sb("dtb", [32, 1], F32)
    aneg = sb("aneg", [32, 1], F32)
    lam_t = T(tmpA.t[:, 0:256], tmpA.b)
    lam_v = sb("lam_v", [128, 4], F32)
    subw_t = sb("subw_t", [128, 128], F32)
    diag = [sb(f"diag{i}", [128, 4, 128], BF16) for i in range(2)]
    dt_t = sb("dt_t", [32, TC], F32)
    a_t = sb("a_t", [32, TC], F32)
    acs_t = sb("acs_t", [32, TC], F32)
    ones32 = sb("ones32", [32, TC], F32)
    dtk = sb("dtk", [128, 32], F32)
    acsk = sb("acsk", [128, 32], F32)
    nacsk = sb("nacsk", [128, 32], F32)
    acsl = sb("acsl", [128, 32], F32)
    etot = sb("etot", [128, 32], F32)
    w2 = sb("w2", [128, 32], F32)
    xcs = sb("xcs", [128, DI], BF16)
    xds = sb("xds", [128, DI], BF16)
    btok = sb("btok", [128, 512], BF16)
    cbm = sb("cbm", [128, 4, 128], BF16)
    rbc = [sb(f"rbc{i}", [128, 8, 128], F32) for i in range(2)]
    rm = sb("rm", [128, 8, 128], F32)
    mts = [sb(f"mt{i}", [128, 8, 128], BF16) for i in range(2)]
    ctss = [sb(f"cts{i}", [128, 8, 128], BF16) for i in range(2)]
    s32 = sb("s32", [128, NH, HP], F32)
    sbf = sb("sbf", [128, NH, HP], BF16)
    s32g = [S.buf(f"s32g{i}") for i in range(4)]
    sbfg = [S.buf(f"sbfg{i}") for i in range(4)]
    stl = T(u.t[:, :, :].rearrange("p a b -> p (a b)")[:, 0:2048].rearrange("p (j n) -> p j n", j=16), u.b)
    convst = sb("convst", [128, 24, 4], F32)
    ctail = sb("ctail", [128, 24, 3], BF16)
    xpre_s = sb("xpre_s", [128, 24, NSAMP, 4], BF16)
    kTh = [sb(f"kTh{i}", [128, LTOT + 112], BF16) for i in range(2)]
    vah = [sb(f"vah{i}", [128, 17, 129], BF16) for i in range(2)]
    et = [sb(f"et{i}", [128, 4, 128], BF16) for i in range(3)]
    et_i = [0]
    ob_i = [0]
    ofin = sb("ofin", [128, 128], F32)
    ofbs = [sb(f"ofb{i}", [128, 128], BF16) for i in range(2)]
    ofb_i = [0]
    epsc = sb("epsc", [128, 1], F32)
    pend_tr = [None]
    sm = sb("sm", [128, 8], F32)
    junk = sb("junk", [128, 128], F32)
    hs1 = sb("hs1", [128, 8, NSAMP], BF16)
    yfm = u
    ytm = ktm

    def dma(q, out, in_, reads, writes, **kw):
        S.add(q, lambda e: e.dma_start(out=out, in_=in_, allow_slow_non_contiguous=True, **kw), reads=reads, writes=writes, dma=True)

    def slow(q, out, in_, reads, writes):
        S.add(q, lambda e: e.dma_start(out=out, in_=in_, allow_slow_non_contiguous=True), reads=reads, writes=writes, dma=True)

    def act(out, in_, func, r, w, bias=None, scale=None, accum=None):
        kw = {}
        if bias is not None:
            kw["bias"] = bias
        if scale is not None:
            kw["scale"] = scale
        if accum is not None:
            kw["accum_out"] = accum
        S.add(ACT, lambda e: e.activation(out=out, in_=in_, func=func, **kw), reads=r, writes=w)

    def acopy(out, in_, r, w):
        S.add(ACT, lambda e: e.copy(out=out, in_=in_), reads=r, writes=w)

    def vcopy(out, in_, r, w):
        S.add(DVE, lambda e: e.tensor_copy(out=out, in_=in_), reads=r, writes=w)

    def tt(out, in0, in1, op, r, w):
        S.add(DVE, lambda e: e.tensor_tensor(out=out, in0=in0, in1=in1, op=op), reads=r, writes=w)

    def ts(out, in0, s1, op0, r, w, s2=None, op1=None):
        if op1 is None:
            S.add(DVE, lambda e: e.tensor_scalar(out=out, in0=in0, scalar1=s1, scalar2=None, op0=op0), reads=r, writes=w)
        else:
            S.add(DVE, lambda e: e.tensor_scalar(out=out, in0=in0, scalar1=s1, scalar2=s2, op0=op0, op1=op1), reads=r, writes=w)

    def stt(out, in0, scalar, in1, op0, op1, r, w):
        S.add(DVE, lambda e: e.scalar_tensor_tensor(out=out, in0=in0, scalar=scalar, in1=in1, op0=op0, op1=op1), reads=r, writes=w)

    def mm(out, lhsT, rhs, start, stop, r, w):
        S.add(PE, lambda e: e.matmul(out, lhsT=lhsT, rhs=rhs, start=start, stop=stop), reads=r, writes=w)

    def tr(out, in_, idn, r, w):
        S.add(PE, lambda e: e.transpose(out, in_, idn), reads=r, writes=w)

    def memset(eng, ap, val, w):
        S.add(eng, lambda e: e.memset(ap, val), writes=w)

    def bcast(t, off, dims):
        return bass.AP(t, off, dims)

    def psb(bank):
        return bank.t[:].bitcast(BF16)

    memset(DVE, ident[:], 0.0, [ident.b])
    S.add(POOL, lambda e: e.affine_select(out=ident[:], in_=ident[:], pattern=[[-1, 128]], compare_op=ALU.not_equal,
                                          fill=1.0, base=0, channel_multiplier=1), reads=[ident.b], writes=[ident.b])
    vcopy(identb[:], ident[:], [ident.b], [identb.b])
    memset(DVE, onesb[:], 1.0, [onesb.b])
    memset(DVE, epsc[:], EPS, [epsc.b])
    memset(DVE, ones32[:], 1.0, [ones32.b])
    dma(POOL, masks[:], masks_in.rearrange("m r j -> r m j"), [], [masks.b])
    for l_ in range(4):
        slow(SP, lnw[:, l_, :], ln_w[l_].rearrange("(k p) o -> p (k o)", p=128), [], [lnw.b])
        slow(SP, lnb[:, l_, :], ln_b[l_].rearrange("(k p) o -> p (k o)", p=128), [], [lnb.b])
    for k_ in range(4):
        slow(SP, convw[:, :, k_], conv_w[k_, :].rearrange("(j p) -> p j", p=128), [], [convw.b])
    slow(SP, convb[:], conv_b.rearrange("(j p) o -> p (j o)", p=128), [], [convb.b])
    slow(SP, normw[:], norm_w.rearrange("(j p) o -> p (j o)", p=128), [], [normw.b])
    dma(SP, dtb[:], dt_bias, [], [dtb.b])
    dma(SP, aneg[:], a_log, [], [aneg.b])
    act(aneg[:], aneg[:], AF.Exp, [aneg.b], [aneg.b])
    ts(aneg[:], aneg[:], -1.0, ALU.mult, [aneg.b], [aneg.b])
    for half in range(2):
        src = bass.AP(d_skip.tensor, half, [[0, 64], [2, 16]])
        slow(SP, dfm[64 * half:64 * half + 64, :], src, [], [dfm.b])
    dma(SP, lam_t[:], bass.AP(lam.tensor, 0, [[0, 128], [1, 256]]), [], [lam_t.b])
    dma(SP, subw_t[:], bass.AP(subw.tensor, 0, [[0, 128], [1, 128]]), [], [subw_t.b])
    ts(subw_t[:], subw_t[:], 1.0 - LAMBDA_INIT, ALU.mult, [subw_t.b], [subw_t.b])
    tt(junk[:, 0:64], lam_t[:, 0:64], lam_t[:, 64:128], ALU.mult, [lam_t.b], [junk.b])
    S.add(DVE, lambda e: e.tensor_reduce(out=lam_v[:, 0:1], in_=junk[:, 0:64], axis=AX.X, op=ALU.add), reads=[junk.b], writes=[lam_v.b])
    tt(junk[:, 64:128], lam_t[:, 128:192], lam_t[:, 192:256], ALU.mult, [lam_t.b], [junk.b])
    S.add(DVE, lambda e: e.tensor_reduce(out=lam_v[:, 1:2], in_=junk[:, 64:128], axis=AX.X, op=ALU.add), reads=[junk.b], writes=[lam_v.b])
    act(lam_v[:, 0:2], lam_v[:, 0:2], AF.Exp, [lam_v.b], [lam_v.b])
    tt(lam_v[:, 2:3], lam_v[:, 0:1], lam_v[:, 1:2], ALU.subtract, [lam_v.b], [lam_v.b])
    ts(lam_v[:, 3:4], lam_v[:, 2:3], LAMBDA_INIT, ALU.add, [lam_v.b], [lam_v.b], s2=-1.0, op1=ALU.mult)

    wscr = {}
    pend_wr = [None]

    def load_w(wdram, r0, nrows, c0, ncols):
        kt = nrows // 128
        wb = wbufs[wb_i[0] % NWB]
        wb_i[0] += 1
        flat = wb.t[:, 0:kt * ncols]
        view = flat.rearrange("p (k c) -> p k c", k=kt)
        key = (wdram.tensor.name, int(wdram.offset), r0, nrows, c0, ncols)
        if key in wscr:
            if pend_wr[0] is not None:
                pend_wr[0]()
                pend_wr[0] = None
            scr = wscr[key]
            dma(POOL, flat, scr.t[:, :], [scr.b], [wb.b])
        else:
            src = wdram[r0:r0 + nrows, c0:c0 + ncols].rearrange("(k p) c -> p k c", p=128)
            dma(POOL, view, src, [], [wb.b])
            scr = dscr(f"wscr{len(wscr)}", [128, kt * ncols], BF16)
            wscr[key] = scr
            prev = pend_wr[0]
            pend_wr[0] = lambda flat=flat, scr=scr, wb=wb: dma(POOL, scr.t[:, :], flat, [wb.b], [scr.b])
            if prev is not None:
                prev()
        return view, wb.b

    def flush_wr():
        if pend_wr[0] is not None:
            pend_wr[0]()
            pend_wr[0] = None

    def rows_to_fm(src_rows, nrows, col0, dst, dstb, extra_reads=()):
        dma(SP, xtm[0:nrows, :], src_rows, list(extra_reads), [xtm.b])
        for half in range(2):
            bk = nb()
            for kk in range(4):
                k = half * 4 + kk
                tr(bk[:, kk * 128:kk * 128 + nrows], xtm[0:nrows, k * 128:(k + 1) * 128], ident[0:nrows, 0:nrows], [xtm.b, ident.b], [bk.b])
            acopy(dst[:, half * 4:half * 4 + 4, col0:col0 + nrows],
                  bk.t[:, :].rearrange("p (k c) -> p k c", k=4)[:, :, 0:nrows], [bk.b], [dstb])

    def linear_fm(wdram, kt, m_total, src, srcb, c0, n, evac, mblk=4):
        nm = (m_total + 127) // 128
        m = 0
        while m < nm:
            nblk = min(mblk, nm - m)
            ncols = min(nblk * 128, m_total - m * 128)
            wv, wvb = load_w(wdram, 0, kt * 128, m * 128, ncols)
            for mi in range(nblk):
                msz = min(128, m_total - (m + mi) * 128)
                bk = nb()
                for k in range(kt):
                    mm(bk[0:msz, 0:n], wv[:, k, mi * 128:mi * 128 + msz], src[:, k, c0:c0 + n], k == 0, k == kt - 1, [wvb, srcb], [bk.b])
                evac(m + mi, bk, msz)
            m += nblk

    def layer_norm(li, c0, n, dst, dstb):
        acopy(ubf[:, :, c0:c0 + n], u[:, :, c0:c0 + n], [u.b], [ubf.b])
        act(usq[:, :, c0:c0 + n], u[:, :, c0:c0 + n], AF.Square, [u.b], [usq.b])
        b1 = nb()
        for k in range(8):
            mm(b1[:, 0:n], onesb[:], ubf[:, k, c0:c0 + n], k == 0, k == 7, [onesb.b, ubf.b], [b1.b])
        b2 = nb()
        for k in range(8):
            mm(b2[:, 0:n], onesb[:], usq[:, k, c0:c0 + n], k == 0, k == 7, [onesb.b, usq.b], [b2.b])
        ts(st1[:, 0:n], b1[:, 0:n], 1.0 / D, ALU.mult, [b1.b], [st1.b])
        tt(st2[:, 0:n], st1[:, 0:n], st1[:, 0:n], ALU.mult, [st1.b], [st2.b])
        stt(st3[:, 0:n], b2[:, 0:n], 1.0 / D, st2[:, 0:n], ALU.mult, ALU.subtract, [b2.b, st2.b], [st3.b])
        ts(st3[:, 0:n], st3[:, 0:n], EPS, ALU.add, [st3.b], [st3.b])
        act(st3[:, 0:n], st3[:, 0:n], AF.Ln, [st3.b], [st3.b])
        act(st3[:, 0:n], st3[:, 0:n], AF.Exp, [st3.b], [st3.b], scale=-0.5)
        uv = u[:, :, c0:c0 + n]
        tt(uv, uv, bcast(st1.t, 0, [[TC, 128], [0, 8], [1, n]]), ALU.subtract, [u.b, st1.b], [u.b])
        tt(uv, uv, bcast(st3.t, 0, [[TC, 128], [0, 8], [1, n]]), ALU.mult, [u.b, st3.b], [u.b])
        for k in range(8):
            act(dst[:, k, c0:c0 + n], u[:, k, c0:c0 + n], AF.Identity, [u.b, lnw.b, lnb.b], [dstb],
                bias=lnb[:, li, k:k + 1], scale=lnw[:, li, k:k + 1])

    def ffn(l, c0, n):
        act_t = bigA
        def ev_gu(f0, nf, wgv, wgb, wuv, wub):
            for fi in range(nf):
                bg = nb()
                for k in range(8):
                    mm(bg[:, 0:n], wgv[:, k, fi * 128:(fi + 1) * 128], hT[:, k, c0:c0 + n], k == 0, k == 7, [wgb, hT.b], [bg.b])
                bu = nb()
                for k in range(8):
                    mm(bu[:, 0:n], wuv[:, k, fi * 128:(fi + 1) * 128], hT[:, k, c0:c0 + n], k == 0, k == 7, [wub, hT.b], [bu.b])
                act(tmpA[:, 0:n], bg[:, 0:n], AF.Silu, [bg.b], [tmpA.b])
                tt(act_t[:, f0 + fi, c0:c0 + n], tmpA[:, 0:n], bu[:, 0:n], ALU.mult, [tmpA.b, bu.b], [act_t.b])
        f = 0
        while f < 22:
            nf = min(4, 22 - f)
            wgv, wgb = load_w(w_g[l], 0, D, f * 128, nf * 128)
            wuv, wub = load_w(w_u[l], 0, D, f * 128, nf * 128)
            ev_gu(f, nf, wgv, wgb, wuv, wub)
            f += nf
        def ev_down(m, bk, msz):
            stt(u[:, m, c0:c0 + n], hT[:, m, c0:c0 + n], ALPHA, bk[:, 0:n], ALU.mult, ALU.add, [hT.b, bk.b], [u.b])
        linear_fm(w_d[l], 22, D, act_t, act_t.b, c0, n, ev_down, mblk=2)

    def linear_tm(wdram, src, srcb, c0, nrows, evac):
        for cbi in range(2):
            wv, wvb = load_w(wdram, 0, D, cbi * 512, 512)
            bk = nb()
            for k in range(8):
                mm(bk[0:nrows, :], src[:, k, c0:c0 + nrows], wv[:, k, :], k == 0, k == 7, [srcb, wvb], [bk.b])
            evac(cbi, bk)

    def linear_tm_multi(wdram, src, srcb, tl, evac, post):
        wvs = [load_w(wdram, 0, D, cbi * 512, 512) for cbi in range(2)]
        for ti, (c0, nr) in enumerate(tl):
            for cbi in range(2):
                wv, wvb = wvs[cbi]
                bk = nb()
                for k in range(8):
                    mm(bk[0:nr, :], src[:, k, c0:c0 + nr], wv[:, k, :], k == 0, k == 7, [srcb, wvb], [bk.b])
                evac(ti, cbi, bk, nr)
            post(ti, c0, nr)

    def rotary(tm, nrows, rope_src):
        dma(SP, rope_t[0:nrows, :], rope_src, [], [rope_t.b])
        v = tm.t[0:nrows, :].rearrange("r (g d) -> r g d", g=16)
        x1 = v[:, :, 0:8]
        x2 = v[:, :, 8:16]
        cos = bcast(rope_t.t, 0, [[16, nrows], [0, 16], [1, 8]])
        sin = bcast(rope_t.t, 8, [[16, nrows], [0, 16], [1, 8]])
        r = rtmp.t
        tt(r[0:nrows, :, :, 0], x1, cos, ALU.mult, [tm.b, rope_t.b], [rtmp.b])
        tt(r[0:nrows, :, :, 1], x2, sin, ALU.mult, [tm.b, rope_t.b], [rtmp.b])
        tt(r[0:nrows, :, :, 2], x2, cos, ALU.mult, [tm.b, rope_t.b], [rtmp.b])
        tt(r[0:nrows, :, :, 3], x1, sin, ALU.mult, [tm.b, rope_t.b], [rtmp.b])
        tt(x1, r[0:nrows, :, :, 0], r[0:nrows, :, :, 1], ALU.subtract, [rtmp.b], [tm.b])
        tt(x2, r[0:nrows, :, :, 2], r[0:nrows, :, :, 3], ALU.add, [rtmp.b], [tm.b])

    def ptt(out, in0, in1, op, r, w):
        S.add(POOL, lambda e: e.tensor_tensor(out=out, in0=in0, in1=in1, op=op), reads=r, writes=w)

    def ssd_sub(c0, L, pb):
        def prep_act(g):
            rb = rbc[g % 2]
            dma(SP, rb[:, :, 0:L], bass.AP(pb.t.tensor, 8 * g * TC + c0, [[0, 128], [TC, 8], [1, L]]), [pb.b], [rb.b])
            tt(rm[0:L, :, 0:L], rb[0:L, :, 0:L], bcast(acsk.t, 8 * g, [[32, L], [1, 8], [0, L]]), ALU.subtract, [rb.b, acsk.b], [rm.b])
            act(rm[0:L, :, 0:L], rm[0:L, :, 0:L], AF.Exp, [rm.b], [rm.b])
            act(rb[:, :, 0:L], rb[:, :, 0:L], AF.Exp, [rb.b], [rb.b])

        def prep_dve(g):
            rb = rbc[g % 2]
            mt_ = mts[g % 2]
            cts_ = ctss[g % 2]
            stt(mt_[0:L, :, 0:L], rm[0:L, :, 0:L], 1.0, bcast(cbm.t, g * 128, [[512, L], [0, 8], [1, L]]), ALU.min, ALU.mult,
                [rm.b, cbm.b], [mt_.b])
            tt(cts_[:, :, 0:L], rb[:, :, 0:L], bcast(CT.t, g * TC + c0, [[4 * TC, 128], [0, 8], [1, L]]), ALU.mult, [rb.b, CT.b], [cts_.b])

        def pework(g):
            mt_ = mts[g % 2]
            cts_ = ctss[g % 2]
            ybk = [nb() for _ in range(4)]
            for pr in range(4):
                for hb in range(2):
                    hh = 2 * pr + hb
                    h = 8 * g + hh
                    mm(ybk[pr][64 * hb:64 * hb + 64, 0:L], xcs[0:L, h * HP:(h + 1) * HP], mt_[0:L, hh, 0:L], True, False, [xcs.b, mt_.b], [ybk[pr].b])
            for pr in range(4):
                for hb in range(2):
                    hh = 2 * pr + hb
                    h = 8 * g + hh
                    mm(ybk[pr][64 * hb:64 * hb + 64, 0:L], sbf[:, h, :], cts_[:, hh, 0:L], False, True, [sbfg[g], cts_.b], [ybk[pr].b])
            for pr in range(4):
                j = 4 * g + pr
                bk = ybk[pr]
                tb_ = tmpBs[pr % 2]
                stt(tb_[:, 0:L], bigC[:, j, c0:c0 + L], dfm[:, j:j + 1], bk[:, 0:L], ALU.mult, ALU.add, [bigC.b, dfm.b, bk.b], [tb_.b])
                tt(bigB[:, j, c0:c0 + L], tb_[:, 0:L], bigB[:, j, c0:c0 + L], ALU.mult, [tb_.b, bigB.b], [bigB.b])
                act(ysq[:, j, c0:c0 + L], bigB[:, j, c0:c0 + L], AF.Square, [bigB.b], [ysq.b])
            bk = nb()
            mm(bk[:, :], btok[0:L, g * 128:(g + 1) * 128], xds[0:L, g * 512:(g + 1) * 512], True, True, [btok.b, xds.b], [bk.b])
            sg = s32[:, 8 * g:8 * g + 8, :]
            tt(sg, sg, bcast(etot.t, 8 * g, [[32, 128], [1, 8], [0, HP]]), ALU.mult, [s32g[g], etot.b], [s32g[g]])
            tt(sg, sg, bk.t[:, :].rearrange("p (h q) -> p h q", h=8), ALU.add, [s32g[g], bk.b], [s32g[g]])
            acopy(sbf[:, 8 * g:8 * g + 8, :], sg, [s32g[g]], [sbfg[g]])

        bk = nb()
        tr(bk[0:L, 0:32], dt_t[:, c0:c0 + L], ident[0:32, 0:32], [dt_t.b, ident.b], [bk.b])
        tr(bk[0:L, 32:64], acs_t[:, c0:c0 + L], ident[0:32, 0:32], [acs_t.b, ident.b], [bk.b])
        vcopy(dtk[0:L, :], bk[0:L, 0:32], [bk.b], [dtk.b])
        vcopy(acsk[0:L, :], bk[0:L, 32:64], [bk.b], [acsk.b])
        ts(nacsk[0:L, :], bk[0:L, 32:64], -1.0, ALU.mult, [bk.b], [nacsk.b])
        slow(SP, acsl[:], bass.AP(pb.t.tensor, c0 + L - 1, [[0, 128], [TC, 32]]), [pb.b], [acsl.b])
        prep_act(0)
        act(etot[:], acsl[:], AF.Exp, [acsl.b], [etot.b])
        tt(w2[0:L, :], acsl[0:L, :], acsk[0:L, :], ALU.subtract, [acsl.b, acsk.b], [w2.b])
        act(w2[0:L, :], w2[0:L, :], AF.Exp, [w2.b], [w2.b])
        for half in range(4):
            bk = nb()
            for jj in range(4):
                j = half * 4 + jj
                tr(psb(bk)[0:L, jj * 128:(jj + 1) * 128], bigC[:, j, c0:c0 + L], identb[:], [bigC.b, identb.b], [bk.b])
            vcopy(xcs[0:L, half * 512:(half + 1) * 512], psb(bk)[0:L, 0:512], [bk.b], [xcs.b])
        bk = nb()
        for g in range(4):
            tr(psb(bk)[0:L, g * 128:(g + 1) * 128], BT[:, g, c0:c0 + L], identb[:], [BT.b, identb.b], [bk.b])
        acopy(btok[0:L, :], psb(bk)[0:L, 0:512], [bk.b], [btok.b])
        xv = xcs.t[0:L, :].rearrange("s (h p) -> s h p", h=NH)
        tt(xv, xv, bcast(dtk.t, 0, [[32, L], [1, 32], [0, HP]]), ALU.mult, [xcs.b, dtk.b], [xcs.b])
        ptt(xds.t[0:L, :].rearrange("s (h p) -> s h p", h=NH), xv, bcast(w2.t, 0, [[32, L], [1, 32], [0, HP]]), ALU.mult,
            [xcs.b, w2.b], [xds.b])
        bk = nb()
        for g in range(4):
            mm(bk[0:L, g * 128:g * 128 + L], BT[:, g, c0:c0 + L], CT[:, g, c0:c0 + L], True, True, [BT.b, CT.b], [bk.b])
        for g in range(4):
            tt(cbm[0:L, g, 0:L], bk[0:L, g * 128:g * 128 + L], masks[0:L, 0, 0:L], ALU.mult, [bk.b, masks.b], [cbm.b])

        prep_dve(0)
        for g in range(4):
            if g + 1 < 4:
                prep_act(g + 1)
            pework(g)
            if g + 1 < 4:
                prep_dve(g + 1)

    def state_out(dst_rows, dstb):
        for q4 in range(4):
            bk = nb()
            for jj in range(4):
                j = q4 * 4 + jj
                tr(bk[:, jj * 128:(jj + 1) * 128], s32.t[:, 2 * j:2 * j + 2, :].rearrange("p h q -> p (h q)"), ident[:], [s32g[j // 4], ident.b], [bk.b])
            vcopy(stl[:, q4 * 4:q4 * 4 + 4, :], bk.t[:, :].rearrange("p (j n) -> p j n", j=4), [bk.b], [stl.b])
        dma(SP, dst_rows.rearrange("(j p) n -> p j n", p=128), stl[:], [stl.b], [dstb])

    def state_in(src_rows):
        dma(SP, stl[:], src_rows.rearrange("(j p) n -> p j n", p=128), [], [stl.b])
        for q4 in range(4):
            bk = nb()
            for jj in range(4):
                j = q4 * 4 + jj
                tr(bk[:, jj * 128:(jj + 1) * 128], stl[:, j, :], ident[:], [stl.b, ident.b], [bk.b])
            vcopy(s32.t[:, 8 * q4:8 * q4 + 8, :].rearrange("p h q -> p (h q)"), bk[:, :], [bk.b], [s32g[q4]])
            acopy(sbf[:, 8 * q4:8 * q4 + 8, :], s32[:, 8 * q4:8 * q4 + 8, :], [s32g[q4]], [sbfg[q4]])

    memset(DVE, bigA[:, :, 0:3], 0.0, [bigA.b])
    for c in range(NCHUNK):
        first = c == 0
        last = c == NCHUNK - 1
        CH = CHS[c]
        np_ = (NMETA + CH) if first else CH
        n = np_ + (NSAMP if first else 0)
        g0 = 0 if first else NMETA + XS0[c]
        pb = acs_scr[c % 2]
        col = 0
        if first:
            rows_to_fm(meta[:, :], NMETA, 0, hT, hT.b)
            col = NMETA
        for i in range(CH // 128):
            rows_to_fm(x_p[XS0[c] + 128 * i:XS0[c] + 128 * i + 128, :], 128, col + 128 * i, hT, hT.b)
        if first:
            rows_to_fm(x_s[:, :], NSAMP, np_, hT, hT.b)
            for s_ in range(NSAMP):
                dma(SP, xtm[0:72, 0:128], st_conv[3 * s_:3 * s_ + 3, :].rearrange("k (j p) -> (k j) p", p=128), [], [xtm.b])
                bk = nb()
                tr(bk[:, 0:72], xtm[0:72, 0:128], ident[0:72, 0:72], [xtm.b, ident.b], [bk.b])
                vcopy(xpre_s[:, :, s_, 0:3], bk.t[:, 0:72].rearrange("p (k j) -> p j k", k=3), [bk.b], [xpre_s.b])
            for s_ in range(NSAMP):
                dma(SP, conv_s[3 * s_:3 * s_ + 2, :], st_conv[3 * s_ + 1:3 * s_ + 3, :], [], [conv_s.b])
        def ev_in(m, bk, msz):
            if m < 16:
                act(bigB[:, m, 0:n], bk[:, 0:n], AF.Silu, [bk.b], [bigB.b])
            elif m < 40:
                j = m - 16
                vcopy(bigA[:, j, 3:3 + np_], bk[:, 0:np_], [bk.b], [bigA.b])
                if first:
                    vcopy(xpre_s[:, j, :, 3], bk[:, np_:n], [bk.b], [xpre_s.b])
                    acopy(convst[:, j, :], bk[:, np_:n], [bk.b], [convst.b])
                if last:
                    acopy(convst[:, j, 0:3], bk[:, np_ - 3:np_], [bk.b], [convst.b])
            else:
                ts(dt_t[:, 0:n], bk[0:32, 0:n], dtb[:, 0:1], ALU.add, [bk.b, dtb.b], [dt_t.b])
                act(a_t[:, 0:n], dt_t[:, 0:n], AF.Abs, [dt_t.b], [a_t.b])
                act(a_t[:, 0:n], a_t[:, 0:n], AF.Exp, [a_t.b], [a_t.b], scale=-1.0)
                act(a_t[:, 0:n], a_t[:, 0:n], AF.Ln, [a_t.b], [a_t.b], bias=1.0)
                stt(dt_t[:, 0:n], dt_t[:, 0:n], 0.0, a_t[:, 0:n], ALU.max, ALU.add, [dt_t.b, a_t.b], [dt_t.b])
                ts(a_t[:, 0:n], dt_t[:, 0:n], aneg[:, 0:1], ALU.mult, [dt_t.b, aneg.b], [a_t.b])
        linear_fm(w_in, 8, INP, hT, hT.b, 0, n, ev_in)
        if first:
            for s_ in range(NSAMP):
                slow(SP, conv_s[3 * s_ + 2, :].rearrange("(j p) -> p j", p=128), convst[:, :, s_], [convst.b], [conv_s.b])
        if last:
            for r_ in range(3):
                slow(SP, conv_p[r_, :].rearrange("(j p) -> p j", p=128), convst[:, :, r_], [convst.b], [conv_p.b])
        if not first:
            vcopy(bigA[:, :, 0:3], ctail[:], [ctail.b], [bigA.b])
        for j in range(24):
            dg = diag[j % 2]
            for k in range(4):
                ts(dg[:, k, :], identb[:], convw[:, j, k:k + 1], ALU.mult, [identb.b, convw.b], [dg.b])
            bk = nb()
            for k in range(4):
                mm(bk[:, 0:np_], dg[:, k, :], bigA[:, j, k:k + np_], k == 0, k == 3, [dg.b, bigA.b], [bk.b])
            if first:
                bk2 = nb()
                for k in range(4):
                    mm(bk2[:, 0:NSAMP], dg[:, k, :], xpre_s[:, j, :, k], k == 0, k == 3, [dg.b, xpre_s.b], [bk2.b])
            if j < 16:
                dst, dstb_, jj = bigC, bigC.b, j
            elif j < 20:
                dst, dstb_, jj = BT, BT.b, j - 16
            else:
                dst, dstb_, jj = CT, CT.b, j - 20
            act(dst[:, jj, 0:np_], bk[:, 0:np_], AF.Silu, [bk.b, convb.b], [dstb_], bias=convb[:, j:j + 1])
            if first:
                act(dst[:, jj, np_:n], bk2[:, 0:NSAMP], AF.Silu, [bk2.b, convb.b], [dstb_], bias=convb[:, j:j + 1])
        vcopy(ctail[:], bigA[:, :, np_:np_ + 3], [bigA.b], [ctail.b])
        subs = []
        if first:
            subs.append((0, NMETA))
            for i in range(CH // 128):
                subs.append((NMETA + 128 * i, 128))
            ssubs = [(np_ + s_, 1) for s_ in range(NSAMP)]
        else:
            for i in range(CH // 128):
                subs.append((128 * i, 128))
            ssubs = []
        for (c0, L) in subs + ssubs:
            S.add(DVE, lambda e, c0=c0, L=L: e.tensor_tensor_scan(out=acs_t[:, c0:c0 + L], data0=ones32[:, c0:c0 + L], data1=a_t[:, c0:c0 + L],
                                                                  initial=0.0, op0=ALU.mult, op1=ALU.add),
                  reads=[ones32.b, a_t.b], writes=[acs_t.b])
        dma(SP, pb.t[:, 0:n], acs_t[:, 0:n], [acs_t.b], [pb.b])
        if first:
            for s_ in range(NSAMP):
                state_in(st_ssm[s_])
                ssd_sub(np_ + s_, 1, pb)
                state_out(ssm_s.t[s_], ssm_s.b)
            memset(DVE, s32[:], 0.0, s32g)
            memset(DVE, sbf[:], 0.0, sbfg)
        for (c0, L) in subs:
            ssd_sub(c0, L, pb)
        if last:
            state_out(ssm_p.t[:, :], ssm_p.b)
        bk = nb()
        for j in range(16):
            mm(bk[:, 0:n], onesb[:], ysq[:, j, 0:n], j == 0, j == 15, [onesb.b, ysq.b], [bk.b])
        ts(st1[:, 0:n], bk[:, 0:n], 1.0 / DI, ALU.mult, [bk.b], [st1.b], s2=EPS, op1=ALU.add)
        act(st1[:, 0:n], st1[:, 0:n], AF.Ln, [st1.b], [st1.b])
        act(st1[:, 0:n], st1[:, 0:n], AF.Exp, [st1.b], [st1.b], scale=-0.5)
        for j in range(16):
            act(bigB[:, j, 0:n], bigB[:, j, 0:n], AF.Copy, [bigB.b, normw.b], [bigB.b], scale=normw[:, j:j + 1])
        def ev_out(m, bk, msz):
            tt(u[:, m, 0:n], bk[:, 0:n], st1[:, 0:n], ALU.mult, [bk.b, st1.b], [u.b])
            stt(u[:, m, 0:n], hT[:, m, 0:n], ALPHA, u[:, m, 0:n], ALU.mult, ALU.add, [hT.b, u.b], [u.b])
        linear_fm(w_out, 16, D, bigB, bigB.b, 0, n, ev_out, mblk=2)
        layer_norm(0, 0, n, hT, hT.b)
        ffn(0, 0, n)
        layer_norm(1, 0, n, hT, hT.b)
        tiles = []
        tiles = [(t_, min(128, np_ - t_)) for t_ in range(0, np_, 128)]
        tl = list(tiles) + ([(np_, NSAMP)] if first else [])

        def ev_kk(ti, cbi, bk, nr):
            tm_ = (ktm, vtm)[ti % 2]
            acopy(tm_[0:nr, cbi * 512:(cbi + 1) * 512], bk[0:nr, :], [bk.b], [tm_.b])

        def post_k(ti, t0, nr):
            tm_ = (ktm, vtm)[ti % 2]
            if ti >= len(tiles):
                rotary(tm_, NSAMP, rope_s[:, :])
                dma(SP, k_s[:, :], tm_[0:NSAMP, :], [tm_.b], [k_s.b])
                return
            gt = g0 + t0
            rotary(tm_, nr, rope_p[gt:gt + nr, :])
            dma(SP, k_p[gt:gt + nr, :], tm_[0:nr, :], [tm_.b], [k_p.b])
            vcopy(ktmb[0:nr, :], tm_[0:nr, :], [tm_.b], [ktmb.b])
            bk = nb()
            for h in range(8):
                tr(psb(bk)[:, h * 128:h * 128 + nr], ktmb[0:nr, h * 128:(h + 1) * 128], identb[0:nr, 0:nr], [ktmb.b, identb.b], [bk.b])
            vcopy(kTt[:, :, 0:nr], psb(bk).rearrange("p (h t) -> p h t", h=8)[:, :, 0:nr], [bk.b], [kTt.b])
            dma(SP, kT_scr.t[:, :, gt:gt + nr].rearrange("h p t -> p h t"), kTt[:, :, 0:nr], [kTt.b], [kT_scr.b])
        linear_tm_multi(w_k, hT, hT.b, tl, ev_kk, post_k)

        def ev_vv(ti, cbi, bk, nr):
            tm_ = (vtm, ktm)[ti % 2]
            acopy(tm_[0:nr, cbi * 512:(cbi + 1) * 512], bk[0:nr, :], [bk.b], [tm_.b])

        def post_v(ti, t0, nr):
            tm_ = (vtm, ktm)[ti % 2]
            if ti >= len(tiles):
                dma(SP, v_s[:, :], tm_[0:NSAMP, :], [tm_.b], [v_s.b])
                return
            gt = g0 + t0
            dma(SP, v_p[gt:gt + nr, :], tm_[0:nr, :], [tm_.b], [v_p.b])
            memset(DVE, vaug[:, :, 128:129], 1.0, [vaug.b])
            vcopy(vaug[0:nr, :, 0:128], tm_.t[0:nr, :].rearrange("r (h e) -> r h e", h=8), [tm_.b], [vaug.b])
            dma(SP, va_scr.t[gt:gt + nr, :], vaug.t[0:nr, :, :].rearrange("r h e -> r (h e)"), [vaug.b], [va_scr.b])
        linear_tm_multi(w_v, hT, hT.b, tl, ev_vv, post_v)
        if first:
            vcopy(hs1[:], hT[:, :, np_:n], [hT.b], [hs1.b])
        if stage < 2:
            continue
        n = np_
        QT = bigC
        def ev_qq(ti, cbi, bk, nr):
            tm_ = (ktm, vtm)[ti % 2]
            acopy(tm_[0:nr, cbi * 512:(cbi + 1) * 512], bk[0:nr, :], [bk.b], [tm_.b])

        def post_q(ti, t0, nr):
            tm_ = (ktm, vtm)[ti % 2]
            if ti >= len(tiles):
                rotary(tm_, NSAMP, rope_s[:, :])
                dma(SP, q_scr.t[:, :], tm_[0:NSAMP, :], [tm_.b], [q_scr.b])
                return
            gt = g0 + t0
            rotary(tm_, nr, rope_p[gt:gt + nr, :])
            vcopy(ktmb[0:nr, :], tm_[0:nr, :], [tm_.b], [ktmb.b])
            bk = nb()
            for h in range(8):
                tr(psb(bk)[:, h * 128:h * 128 + nr], ktmb[0:nr, h * 128:(h + 1) * 128], identb[0:nr, 0:nr], [ktmb.b, identb.b], [bk.b])
            vcopy(QT[:, 0:8, t0:t0 + nr], psb(bk).rearrange("p (h t) -> p h t", h=8)[:, :, 0:nr], [bk.b], [QT.b])
        linear_tm_multi(w_q, hT, hT.b, tl, ev_qq, post_q)
        gend = g0 + np_
        nkt = (gend + 127) // 128
        attnT = bigB
        nb_pool[0] = 4
        for h in range(8):
            kh = kTh[h % 2]
            vh = vah[h % 2]
            dma(SP, kh[:, 0:gend], kT_scr.t[h, :, 0:gend], [kT_scr.b], [kh.b])
            nfull = gend // 128
            rem = gend - 128 * nfull
            dma(SP, vh[:, 0:nfull, :], va_scr.t[0:nfull * 128, h * 129:(h + 1) * 129].rearrange("(t p) e -> p t e", p=128), [va_scr.b], [vh.b])
            if rem:
                dma(SP, vh[0:rem, nfull, :], va_scr.t[nfull * 128:gend, h * 129:(h + 1) * 129], [va_scr.b], [vh.b])
            for (t0, nr) in tiles:
                a = g0 + t0
                b_ = a + nr
                kts = [kt for kt in range(nkt) if 128 * kt < b_]
                ob = [banks[4 + 2 * (ob_i[0] % 2)], banks[5 + 2 * (ob_i[0] % 2)]]
                ob_i[0] += 1
                items = [(cc, kts[gi:gi + 4]) for cc in range(2) for gi in range(0, len(kts), 4)]

                def stage_a(ix):
                    cc, grp = items[ix]
                    sbk = nb()
                    for ii, kt in enumerate(grp):
                        mk = min(128, gend - 128 * kt)
                        mm(sbk[0:mk, ii * 128:ii * 128 + nr], kh[64 * cc:64 * cc + 64, kt * 128:kt * 128 + mk],
                           QT[64 * cc:64 * cc + 64, h, t0:t0 + nr], True, True, [kh.b, QT.b], [sbk.b])
                    e_ = et[et_i[0] % 3]
                    et_i[0] += 1
                    act(e_[:, 0:len(grp), 0:nr], sbk.t[:, :].rearrange("p (i q) -> p i q", i=4)[:, 0:len(grp), 0:nr], AF.Exp,
                        [sbk.b], [e_.b], scale=SCALE)
                    for ii, kt in enumerate(grp):
                        delta = a - 128 * kt
                        if delta < 127:
                            mi = {0: 0, 16: 1, -112: 2}[delta]
                            tt(e_[:, ii, 0:nr], e_[:, ii, 0:nr], masks[:, mi, 0:nr], ALU.mult, [e_.b, masks.b], [e_.b])
                    return e_

                def stage_b(ix, e_):
                    cc, grp = items[ix]
                    for ii, kt in enumerate(grp):
                        mk = min(128, gend - 128 * kt)
                        mm(ob[cc][0:nr, 0:129], e_[0:mk, ii, 0:nr], vh[0:mk, kt, :], kt == kts[0], kt == kts[-1], [e_.b, vh.b], [ob[cc].b])

                pend = stage_a(0)
                steps = []
                if pend_tr[0] is not None:
                    steps = pend_tr[0]()
                    pend_tr[0] = None
                for ix in range(len(items)):
                    nxt = stage_a(ix + 1) if ix + 1 < len(items) else None
                    stage_b(ix, pend)
                    pend = nxt
                    if steps and ix < 3:
                        steps.pop(0)()
                for st_ in steps:
                    st_()
                def _tail_steps(ob=ob, h=h, t0=t0, nr=nr):
                    ofb = ofbs[ofb_i[0] % 2]
                    ofb_i[0] += 1

                    def s_dve():
                        S.add(DVE, lambda e: e.reciprocal(out=sm[0:nr, 0:1], in_=ob[0][0:nr, 128:129]), reads=[ob[0].b], writes=[sm.b])
                        S.add(DVE, lambda e: e.reciprocal(out=sm[0:nr, 1:2], in_=ob[1][0:nr, 128:129]), reads=[ob[1].b], writes=[sm.b])
                        tt(sm[0:nr, 1:2], sm[0:nr, 1:2], lam_v[0:nr, 3:4], ALU.mult, [sm.b, lam_v.b], [sm.b])
                        ts(ofin[0:nr, :], ob[0][0:nr, 0:128], sm[0:nr, 0:1], ALU.mult, [ob[0].b, sm.b], [ofin.b])
                        stt(ofin[0:nr, :], ob[1][0:nr, 0:128], sm[0:nr, 1:2], ofin[0:nr, :], ALU.mult, ALU.add, [ob[1].b, sm.b, ofin.b], [ofin.b])

                    def s_act():
                        act(junk[0:nr, :], ofin[0:nr, :], AF.Square, [ofin.b], [junk.b, sm.b], accum=sm[0:nr, 2:3])
                        act(sm[0:nr, 2:3], sm[0:nr, 2:3], AF.Ln, [sm.b, epsc.b], [sm.b], scale=1.0 / 128, bias=epsc[0:nr, 0:1])
                        act(sm[0:nr, 2:3], sm[0:nr, 2:3], AF.Exp, [sm.b], [sm.b], scale=-0.5)

                    def s_fin():
                        stt(ofb[0:nr, :], ofin[0:nr, :], sm[0:nr, 2:3], subw_t[0:nr, :], ALU.mult, ALU.mult, [ofin.b, sm.b, subw_t.b], [ofb.b])

                    def s_tr():
                        tb = nb()
                        tr(psb(tb)[:, 0:nr], ofb[0:nr, :], identb[0:nr, 0:nr], [ofb.b, identb.b], [tb.b])
                        vcopy(attnT[:, h, t0:t0 + nr], psb(tb)[:, 0:nr], [tb.b], [attnT.b])
                    return [s_dve, s_act, s_fin, s_tr]
                pend_tr[0] = _tail_steps
        if pend_tr[0] is not None:
            for st_ in pend_tr[0]():
                st_()
            pend_tr[0] = None
        nb_pool[0] = 8

        def ev_o(m, bk, msz):
            stt(u[:, m, 0:n], hT[:, m, 0:n], ALPHA, bk[:, 0:n], ALU.mult, ALU.add, [hT.b, bk.b], [u.b])
        linear_fm(w_o, 8, D, attnT, attnT.b, 0, n, ev_o)
        layer_norm(2, 0, n, hT, hT.b)
        ffn(1, 0, n)
        layer_norm(3, 0, n, yfm, yfm.b)
        for (t0, nr) in tiles:
            for half in range(2):
                bk = nb()
                for kk in range(4):
                    k = half * 4 + kk
                    tr(bk[0:nr, kk * 128:(kk + 1) * 128], yfm[:, k, t0:t0 + nr], ident[:], [yfm.b, ident.b], [bk.b])
                acopy(ytm[0:nr, half * 512:(half + 1) * 512], bk[0:nr, :], [bk.b], [ytm.b])
            gt = g0 + t0
            lo = max(gt, NMETA)
            hi = gt + nr
            if hi > lo:
                dma(SP, y_p[lo - NMETA:hi - NMETA, :], ytm[lo - gt:nr, :], [ytm.b], [y_p.b])


    if stage >= 3:
        flatA = bigA.t[:, :, :].rearrange("p a b -> p (a b)")
        flatB = bigB.t[:, :, :].rearrange("p a b -> p (a b)")
        flatC = bigC.t[:, :, :].rearrange("p a b -> p (a b)")
        W4 = TT_ * D
        KTs = [T(flatA[:, i * W4:(i + 1) * W4].rearrange("p (t d) -> p t d", t=TT_), S.buf(f"Kt{i}")) for i in range(2)]
        VTs = [T(flatB[:, 0:W4].rearrange("p (t d) -> p t d", t=TT_), S.buf("Vt0")),
               T(flatC[:, 0:W4].rearrange("p (t d) -> p t d", t=TT_), S.buf("Vt1"))]
        PRs = [T(wbufs[i].t[:, 0:W4].rearrange("p (t d) -> p t d", t=TT_), S.buf(f"prod{i}")) for i in range(2)]
        parents = {id(KTs[0]): [bigA.b], id(KTs[1]): [bigA.b], id(VTs[0]): [bigB.b], id(VTs[1]): [bigC.b],
                   id(PRs[0]): [wbufs[0].b], id(PRs[1]): [wbufs[1].b]}
        used = set()

        def wv(view):
            if id(view) in used:
                return [view.b]
            used.add(id(view))
            return [view.b] + parents[id(view)]
        esum = sb("esum", [128, 16], F32)
        esumb = sb("esumb", [128, 8, 4], F32)
        etile = sb("etile", [128, 16], F32)
        ones1 = sb("ones1", [128, 1], F32)
        den = sb("den", [4, 8], F32)
        idxt = sb("idxt", [128, 1], I32)
        idx2 = sb("idx2", [128, NTT], I32)
        qbc = T(xcs.t[:, 0:D], xcs.b)
        selp = sb("selp_sb", [128, 2], F32)
        cmask = sb("cmask_sb", [4, 2], F32)
        combc = sb("combc", [4, 2, 2], F32)
        comb = sb("comb_sb", [4, 2], F32)
        scs = [sb(f"sc{i}", [128, TT_, 16], F32) for i in range(2)]
        eblks = [sb(f"eblk{i}", [128, TT_, 8, 4], BF16) for i in range(2)]
        oacc = T(xds.t[:, :].bitcast(F32)[0:4, :].rearrange("m (h e) -> m h e", h=8), xds.b)
        uflat = u.t[:, :, :].rearrange("p a b -> p (a b)")
        pn = T(uflat[0:4, 0:1024].rearrange("m (g d) -> m g d", g=16), u.b)
        sn = sb("sn", [4, 16], F32)
        sn2 = sb("sn2", [4, 8], F32)
        on = T(vtm.t[0:4, :], vtm.b)
        o2 = T(uflat[0:2, 0:1024], u.b)
        o2s = T(uflat[0:2, 1024:2048], u.b)
        r2 = sb("r2", [2, 8], F32)
        attnT_s = sb("attnT_s", [128, 8, NSAMP], BF16)
        dma(SP, selp[:], selp_in, [], [selp.b])
        dma(SP, cmask[:], cmask_in, [], [cmask.b])
        dma(SP, combc[:], comb_in.rearrange("a m b -> m a b"), [], [combc.b])
        stt(comb[:], combc[:, 1, :], lam_v[0:4, 3:4], combc[:, 0, :], ALU.mult, ALU.add, [combc.b, lam_v.b], [comb.b])
        memset(DVE, ones1[:], 1.0, [ones1.b])
        for pr in range(2):
            dma(SP, idxt[:], ptab[128 * pr:128 * pr + 128, :], [], [idxt.b])
            for ti in range(NTT):
                ts(idx2[:, ti:ti + 1], idxt[:, 0:1], float(NTT), ALU.mult, [idxt.b], [idx2.b], s2=float(ti), op1=ALU.add)
            for b2 in range(2):
                dma(POOL, qbc[64 * b2:64 * b2 + 64, :], bass.AP(q_scr.t.tensor, (2 * pr + b2) * D, [[0, 64], [1, D]]), [q_scr.b], [qbc.b])
            memset(DVE, oacc[:], 0.0, [oacc.b])
            memset(DVE, esum[:], 0.0, [esum.b])
            for ti in range(NTT):
                Kt, Vt, prod, sc, eblk = KTs[ti % 2], VTs[ti % 2], PRs[ti % 2], scs[ti % 2], eblks[ti % 2]
                S.add(POOL, lambda e, ti=ti, Kt=Kt: e.indirect_dma_start(out=Kt.t.rearrange("p t d -> p (t d)"), out_offset=None, in_=cache_k[:, :],
                                                                          in_offset=bass.IndirectOffsetOnAxis(ap=idx2[:, ti:ti + 1], axis=0)),
                      reads=[idx2.b], writes=wv(Kt), dma=True)
                S.add(POOL, lambda e, ti=ti, Vt=Vt: e.indirect_dma_start(out=Vt.t.rearrange("p t d -> p (t d)"), out_offset=None, in_=cache_v[:, :],
                                                                          in_offset=bass.IndirectOffsetOnAxis(ap=idx2[:, ti:ti + 1], axis=0)),
                      reads=[idx2.b], writes=wv(Vt), dma=True)
                tt(prod[:, :, :], Kt[:, :, :], bcast(xcs.t, 0, [[DI, 128], [0, TT_], [1, D]]), ALU.mult, [Kt.b, qbc.b], wv(prod))
                S.add(DVE, lambda e, prod=prod, sc=sc: e.tensor_reduce(out=sc[:, :, :], in_=prod.t.rearrange("p t (g d) -> p t g d", g=16), axis=AX.X, op=ALU.add),
                      reads=[prod.b], writes=[sc.b])
                act(sc[:, :, :], sc[:, :, :], AF.Exp, [sc.b], [sc.b], scale=SCALE)
                S.add(DVE, lambda e, sc=sc: e.tensor_reduce(out=etile[:, :], in_=sc.t[:, :, :].rearrange("p t g -> p g t"), axis=AX.X, op=ALU.add),
                      reads=[sc.b], writes=[etile.b])
                tt(esum[:, :], esum[:, :], etile[:, :], ALU.add, [esum.b, etile.b], [esum.b])
                scv = sc.t[:, :, :].rearrange("p t (h c) -> p t h c", h=8)
                for b2 in range(2):
                    ts(eblk.t[:, :, :, :].rearrange("p t h (c b) -> p t h c b", c=2)[:, :, :, :, b2],
                       scv, selp[:, b2:b2 + 1], ALU.mult, [sc.b, selp.b], [eblk.b])
                pbk = [nb(), nb()]
                for h in range(8):
                    o = pbk[h // 4][0:4, (h % 4) * 128:(h % 4) * 128 + 128]
                    for t in range(TT_):
                        mm(o, eblk[:, t, h, :], Vt[:, t, h * 128:(h + 1) * 128], t == 0, t == TT_ - 1, [eblk.b, Vt.b], [pbk[h // 4].b])
                for bi in range(2):
                    ov = oacc.t[:, 4 * bi:4 * bi + 4, :].rearrange("m h e -> m (h e)")
                    tt(ov, ov, pbk[bi][0:4, :], ALU.add, [oacc.b, pbk[bi].b], [oacc.b])
            esv = esum.t[:, :].rearrange("p (h c) -> p h c", h=8)
            for b2 in range(2):
                ts(esumb.t[:, :, :].rearrange("p h (c b) -> p h c b", c=2)[:, :, :, b2], esv, selp[:, b2:b2 + 1], ALU.mult, [esum.b, selp.b], [esumb.b])
            bk = nb()
            for h in range(8):
                mm(bk[0:4, h:h + 1], esumb[:, h, :], ones1[:, :], True, True, [esumb.b, ones1.b], [bk.b])
            vcopy(den[:, :], bk[0:4, 0:8], [bk.b], [den.b])
            for cc in range(2):
                dma(SP, ktm[2 * cc:2 * cc + 2, :], q_scr.t[2 * pr:2 * pr + 2, :], [q_scr.b], [ktm.b])
                dma(SP, vtm[2 * cc:2 * cc + 2, :], k_s.t[2 * pr:2 * pr + 2, :], [k_s.b], [vtm.b])
            tt(pn[:, :, :], ktm.t[0:4, :].rearrange("m (g d) -> m g d", g=16), vtm.t[0:4, :].rearrange("m (g d) -> m g d", g=16), ALU.mult, [ktm.b, vtm.b], [pn.b])
            for cc in range(2):
                dma(SP, ktm[2 * cc:2 * cc + 2, :], v_s.t[2 * pr:2 * pr + 2, :], [v_s.b], [ktm.b])
            S.add(DVE, lambda e: e.tensor_reduce(out=sn[:, :], in_=pn[:, :, :], axis=AX.X, op=ALU.add), reads=[pn.b], writes=[sn.b])
            act(sn[:, :], sn[:, :], AF.Exp, [sn.b], [sn.b], scale=SCALE)
            snv = sn.t[:, :].rearrange("m (h c) -> m h c", h=8)
            tt(snv, snv, bcast(cmask.t, 0, [[2, 4], [0, 8], [1, 2]]), ALU.mult, [sn.b, cmask.b], [sn.b])
            S.add(DVE, lambda e: e.tensor_reduce(out=sn2[:, :], in_=sn.t[:, :].rearrange("m (h c) -> m h c", h=8), axis=AX.X, op=ALU.add), reads=[sn.b], writes=[sn2.b])
            tt(on.t[:, :].rearrange("m (h e) -> m h e", h=8), ktm.t[0:4, :].rearrange("m (h e) -> m h e", h=8),
               bcast(sn2.t, 0, [[8, 4], [1, 8], [0, 128]]), ALU.mult, [ktm.b, sn2.b], [on.b])
            tt(oacc[:, :, 0:128], oacc[:, :, 0:128], on.t[:, :].rearrange("m (h e) -> m h e", h=8), ALU.add, [oacc.b, on.b], [oacc.b])
            tt(den[:, :], den[:, :], sn2[:, :], ALU.add, [den.b, sn2.b], [den.b])
            S.add(DVE, lambda e: e.reciprocal(out=sn2[:, :], in_=den[:, :]), reads=[den.b], writes=[sn2.b])
            tt(on.t[:, :].rearrange("m (h e) -> m h e", h=8), oacc[:, :, 0:128], bcast(sn2.t, 0, [[8, 4], [1, 8], [0, 128]]), ALU.mult,
               [oacc.b, sn2.b], [on.b])
            for half in range(2):
                bk = nb()
                mm(bk[0:2, :], comb[:, :], on[:, half * 512:(half + 1) * 512], True, True, [comb.b, on.b], [bk.b])
                vcopy(o2[:, half * 512:(half + 1) * 512], bk[0:2, :], [bk.b], [o2.b])
            tt(o2s[:, :], o2[:, :], o2[:, :], ALU.mult, [o2.b], [o2s.b])
            S.add(DVE, lambda e: e.tensor_reduce(out=r2[:, :], in_=o2s.t[:, :].rearrange("b (h e) -> b h e", h=8), axis=AX.X, op=ALU.add), reads=[o2s.b], writes=[r2.b])
            ts(r2[:, :], r2[:, :], 1.0 / 128, ALU.mult, [r2.b], [r2.b], s2=EPS, op1=ALU.add)
            act(r2[:, :], r2[:, :], AF.Ln, [r2.b], [r2.b])
            act(r2[:, :], r2[:, :], AF.Exp, [r2.b], [r2.b], scale=-0.5)
            o2v = o2.t[:, :].rearrange("b (h e) -> b h e", h=8)
            tt(o2v, o2v, bcast(r2.t, 0, [[8, 2], [1, 8], [0, 128]]), ALU.mult, [o2.b, r2.b], [o2.b])
            tt(o2v, o2v, bcast(subw_t.t, 0, [[128, 2], [0, 8], [1, 128]]), ALU.mult, [o2.b, subw_t.b], [o2.b])
            bk = nb()
            for h in range(8):
                tr(bk[:, 2 * h:2 * h + 2], o2[0:2, h * 128:(h + 1) * 128], ident[0:2, 0:2], [o2.b, ident.b], [bk.b])
            vcopy(attnT_s[:, :, 2 * pr:2 * pr + 2], bk.t[:, 0:16].rearrange("p (h b) -> p h b", h=8), [bk.b], [attnT_s.b])
        S.add(DVE, lambda e: e.memset(ones1[:], 1.0), reads=[v.b for v in KTs + VTs + PRs],
              writes=[ones1.b, bigA.b, bigB.b, bigC.b, wbufs[0].b, wbufs[1].b])
        n = NSAMP
        vcopy(hT[:, :, 0:n], hs1[:], [hs1.b], [hT.b])
        def ev_os(m, bk, msz):
            stt(u[:, m, 0:n], hT[:, m, 0:n], ALPHA, bk[:, 0:n], ALU.mult, ALU.add, [hT.b, bk.b], [u.b])
        linear_fm(w_o, 8, D, attnT_s, attnT_s.b, 0, n, ev_os)
        layer_norm(2, 0, n, hT, hT.b)
        ffn(1, 0, n)
        layer_norm(3, 0, n, yfm, yfm.b)
        for half in range(2):
            bk = nb()
            for kk in range(4):
                k = half * 4 + kk
                tr(bk[0:n, kk * 128:(kk + 1) * 128], yfm[:, k, 0:n], ident[:], [yfm.b, ident.b], [bk.b])
            acopy(ytm[0:n, half * 512:(half + 1) * 512], bk[0:n, :], [bk.b], [ytm.b])
        dma(SP, y_s[:, :], ytm[0:n, :], [ytm.b], [y_s.b])

    flush_wr()
    finals = [k_p.b, v_p.b, ssm_p.b, conv_p.b, k_s.b, v_s.b, ssm_s.b, conv_s.b, y_p.b, y_s.b]
    S.emit(final_bufs=finals)
    st.close()
    return nc


def _host_consts():
    half = 8
    inv = (500000.0 ** (-(np.arange(half, dtype=np.float32) * 2.0 / 16))).astype(np.float32)
    def tab(pos):
        ang = pos.astype(np.float32)[:, None] * inv[None, :]
        return np.concatenate([np.cos(ang), np.sin(ang)], 1).astype(np.float32)
    rope_p = tab(np.arange(LTOT))
    rope_s = tab(np.full((NSAMP,), PAST))
    r = np.arange(128)[:, None]
    j = np.arange(128)[None, :]
    masks = np.stack([(r - j <= 0), (r - j <= 16), (r - j <= -112)]).astype(np.float32)
    selq = np.zeros((2, NSAMP, 128), np.float32)
    rowsel = np.zeros((2, NSAMP, 4), np.float32)
    for pr in range(2):
        for p in range(128):
            selq[pr, 2 * pr + p // 64, p] = 1.0
        for m_ in range(4):
            rowsel[pr, 2 * pr + (m_ % 2), m_] = 1.0
    selp = np.zeros((128, 2), np.float32)
    selp[:64, 0] = 1.0
    selp[64:, 1] = 1.0
    cmask = np.zeros((4, 2), np.float32)
    comb = np.zeros((2, 4, 2), np.float32)
    for m_ in range(4):
        cmask[m_, m_ // 2] = 1.0
        comb[m_ // 2, m_, m_ % 2] = 1.0
    return rope_p, rope_s, masks, dict(selp=selp, cmask=cmask, comb=comb)


def kernel(x_prompt, x_sample, cache_k, cache_v, state_ssm, state_conv, page_table, meta_tokens,
           a_w_in, a_conv_w, a_conv_b, a_dt_bias, a_A_log, a_D, a_norm_w, a_w_out,
           kv_w_k, kv_w_v, b_w_q, b_lambda, b_subln_w, b_w_o,
           ffn_w_gate, ffn_w_up, ffn_w_down, ln_mix_w, ln_mix_b, ln_ffn_w, ln_ffn_b, _stage=99):
    f = lambda a: np.ascontiguousarray(np.asarray(a))
    rope_p, rope_s, masks, dconst = _host_consts()
    percore = isinstance(cache_k, list)
    npool = int(np.asarray(cache_k[0] if percore else cache_k).shape[0])
    nc = build_program(_stage, npool)
    ck = cv = None
    if _stage >= 3 and not percore:
        ck = f(cache_k).reshape(npool * 32, 4 * D)
        cv = f(cache_v).reshape(npool * 32, 4 * D)
    ln_w = np.stack([f(ln_mix_w)[0], f(ln_ffn_w)[0], f(ln_mix_w)[1], f(ln_ffn_w)[1]]).reshape(4, D, 1)
    ln_b = np.stack([f(ln_mix_b)[0], f(ln_ffn_b)[0], f(ln_mix_b)[1], f(ln_ffn_b)[1]]).reshape(4, D, 1)
    shared = dict(
        meta=f(meta_tokens),
        w_in=f(a_w_in)[0], conv_w=f(a_conv_w)[0], conv_b=f(a_conv_b)[0].reshape(CONVD, 1),
        dt_bias=f(a_dt_bias)[0].reshape(NH, 1), a_log=f(a_A_log)[0].reshape(NH, 1), d_skip=f(a_D)[0].reshape(NH, 1),
        norm_w=f(a_norm_w)[0].reshape(DI, 1), w_out=f(a_w_out)[0], w_k=f(kv_w_k), w_v=f(kv_w_v), w_q=f(b_w_q)[0],
        lam=f(b_lambda)[0].reshape(1, 256), subw=f(b_subln_w)[0].reshape(1, 128), w_o=f(b_w_o)[0],
        w_g=f(ffn_w_gate), w_u=f(ffn_w_up), w_d=f(ffn_w_down), ln_w=ln_w, ln_b=ln_b,
        rope_p=rope_p, rope_s=rope_s, masks=masks,
    )
    if _stage >= 3:
        shared.update(dconst)
        if not percore:
            shared["cache_k"] = ck
            shared["cache_v"] = cv
    xp = f(x_prompt)
    xs = f(x_sample).reshape(32, D)
    ss = f(state_ssm).reshape(32, NH * HP, NS)
    sc = f(state_conv).reshape(32, 3, CONVD)
    pt = f(page_table).astype(np.int32) if not percore else None
    in_maps = []
    for c in range(NCORES):
        m = dict(shared)
        m["x_p"] = xp[c]
        m["x_s"] = xs[4 * c:4 * c + 4]
        m["st_ssm"] = ss[4 * c:4 * c + 4]
        m["st_conv"] = sc[4 * c:4 * c + 4].reshape(12, CONVD)
        if _stage >= 3 and not percore:
            m["ptab"] = pt[4 * c:4 * c + 4].reshape(NSAMP * NPAGES, 1)
        if _stage >= 3 and percore:
            m["ptab"] = np.asarray(page_table[c], np.int32).reshape(NSAMP * NPAGES, 1)
            m["cache_k"] = f(cache_k[c]).reshape(npool * 32, 4 * D)
            m["cache_v"] = f(cache_v[c]).reshape(npool * 32, 4 * D)
        in_maps.append(m)
    res = run_bass_kernel_spmd(nc, in_maps, core_ids=list(range(NCORES))).results
    cat = lambda k: np.stack([r[k] for r in res])
    y_prompt = cat("y_p").reshape(8, SEQ, D)
    y_sample = cat("y_s").reshape(32, 1, D)
    k_prompt = cat("k_p").reshape(8, LTOT, 8, 2, 64)
    v_prompt = cat("v_p").reshape(8, LTOT, 8, 128)
    ssm_prompt = cat("ssm_p").reshape(1, 8, NH, HP, NS)
    conv_prompt = cat("conv_p").reshape(1, 8, 3, CONVD)
    k_sample = cat("k_s").reshape(32, 1, 8, 2, 64)
    v_sample = cat("v_s").reshape(32, 1, 8, 128)
    ssm_sample = cat("ssm_s").reshape(1, 32, NH, HP, NS)
    conv_sample = cat("conv_s").reshape(1, 32, 3, CONVD)
    return (y_prompt, y_sample, k_prompt, v_prompt, ssm_prompt, conv_prompt, k_sample, v_sample, ssm_sample, conv_sample)
```
